# Optimizing a Trainium2 kernel written in Bass

```python
import math
import jax
import jax.numpy as jnp
from jax import lax
import numpy as np

D_MODEL = 4096
BATCH = 4
SEQ = 2048
DEPTH = 2
DEC_BATCH = 8
DEC_SEQ = 4
PAST_LEN = 16384
PAGE_SIZE = 128

A_CHUNK = 128
A_GROUPS = 8
A_WIDTH = D_MODEL // 2
A_GROUP_DIM = A_WIDTH // A_GROUPS
B_HEADS = 16
B_HEAD_DIM = 128
B_WIDTH = B_HEADS * B_HEAD_DIM
B_PATTERNS = ((128, 1), (512, 4), (2048, 16))
B_WIN_MAX = max(w for w, _ in B_PATTERNS)
B_BLOCK = 128
C_HEADS = 8
C_QK_DIM = 128
C_V_DIM = 256
C_QK_WIDTH = C_HEADS * C_QK_DIM
C_V_WIDTH = C_HEADS * C_V_DIM
C_CHUNK = 64
F_BIAS_INIT = 3.0
N_BRANCH = 3
D_FF = 4 * D_MODEL
EPS = 1e-6
NEG_INF = -1e30

SPLIT_SIZES = (A_WIDTH, A_WIDTH, B_WIDTH, B_WIDTH, B_WIDTH, C_QK_WIDTH, C_QK_WIDTH,
               C_V_WIDTH, C_V_WIDTH, C_HEADS, C_HEADS, N_BRANCH * D_MODEL)
SPLIT_POINTS = tuple(int(v) for v in np.cumsum(SPLIT_SIZES)[:-1])
D_IN = sum(SPLIT_SIZES)

kernel_name = 'hybrid_gated_gmlp_dilswa_mlstm_step'


def _rms_norm(x, g):
    xf = x.astype(jnp.float32)
    y = xf * lax.rsqrt(jnp.mean(xf * xf, axis=-1, keepdims=True) + EPS)
    return (y * g.astype(jnp.float32)).astype(x.dtype)


def _chunk_spatial_gate(u, v, w_s, b_s):
    n, s = v.shape[:2]
    pad = (-s) % A_CHUNK
    nc = (s + pad) // A_CHUNK
    vc = jnp.pad(v, ((0, 0), (0, pad), (0, 0), (0, 0))).reshape(n, nc, A_CHUNK, A_GROUPS, A_GROUP_DIM)
    causal = jnp.tril(jnp.ones((A_CHUNK, A_CHUNK), dtype=bool))
    w = jnp.where(causal, w_s, 0.0)
    mixed = jnp.einsum('gts,ncsgd->nctgd', w, vc) + b_s.T[None, None, :, :, None]
    mixed = mixed.reshape(n, nc * A_CHUNK, A_GROUPS, A_GROUP_DIM)[:, :s]
    return u * mixed


def _masked_probs(scores, valid):
    sc = jnp.where(valid, scores, NEG_INF)
    m = jnp.max(sc, axis=-1)
    p = jnp.where(valid, jnp.exp(sc - m[..., None]), 0.0)
    return m, p


def _combine_by_denominator(ms, ss, nums):
    m = jnp.stack(ms)
    s = jnp.stack(ss)
    num = jnp.stack(nums)
    w = jax.nn.softmax(m + jnp.log(s), axis=0)
    return jnp.sum(w[..., None] * num / s[..., None], axis=0)


def _dilated_attn_prompt(q, k, v):
    n, s, h, dh = q.shape
    ms, ss, nums = [], [], []
    for win, dil in B_PATTERNS:
        span = win // dil
        ln = s // dil
        lp = -(-ln // B_BLOCK) * B_BLOCK
        nb = lp // B_BLOCK

        def strided(t):
            t = t.reshape(n, ln, dil, h, dh)
            t = jnp.pad(t, ((0, 0), (0, lp - ln), (0, 0), (0, 0), (0, 0)))
            return t.reshape(n, nb, B_BLOCK, dil, h, dh)

        qb, kb, vb = strided(q), strided(k), strided(v)
        pad_blk = ((0, 0), (1, 0), (0, 0), (0, 0), (0, 0), (0, 0))
        kc = jnp.concatenate([jnp.pad(kb, pad_blk)[:, :-1], kb], axis=2)
        vc = jnp.concatenate([jnp.pad(vb, pad_blk)[:, :-1], vb], axis=2)
        scores = jnp.einsum('nbqrhd,nbkrhd->nbrhqk', qb, kc, preferred_element_type=jnp.float32)
        qi = jnp.arange(B_BLOCK)[:, None]
        kj = jnp.arange(2 * B_BLOCK)[None, :]
        dist = qi + B_BLOCK - kj
        blk = jnp.arange(nb)[:, None, None]
        valid = (dist >= 0) & (dist <= span) & ((blk > 0) | (kj >= B_BLOCK))
        m, p = _masked_probs(scores, valid[None, :, None, None])
        den = jnp.sum(p, axis=-1)
        num = jnp.einsum('nbrhqk,nbkrhd->nbqrhd', p, vc.astype(jnp.float32))

        def unblock(t):
            t = t.reshape((n, lp, dil, h) + t.shape[5:])[:, :ln]
            return t.reshape((n, s, h) + t.shape[4:])

        ms.append(unblock(jnp.transpose(m, (0, 1, 4, 2, 3))))
        ss.append(unblock(jnp.transpose(den, (0, 1, 4, 2, 3))))
        nums.append(unblock(num))
    return _combine_by_denominator(ms, ss, nums)


def _dilated_attn_sample(q, k_all, v_all):
    t = q.shape[1]
    buf = k_all.shape[1] - t
    tq = jnp.arange(t)
    ms, ss, nums = [], [], []
    for win, dil in B_PATTERNS:
        j = jnp.arange(win // dil + 1)
        idx = buf + tq[:, None] - j[None, :] * dil
        valid = idx >= 0
        idx = jnp.maximum(idx, 0)
        kg = k_all[:, idx]
        vg = v_all[:, idx]
        scores = jnp.einsum('nthd,ntjhd->nhtj', q, kg, preferred_element_type=jnp.float32)
        m, p = _masked_probs(scores, valid[None, None])
        den = jnp.sum(p, axis=-1)
        num = jnp.einsum('nhtj,ntjhd->nthd', p, vg.astype(jnp.float32))
        ms.append(jnp.transpose(m, (0, 2, 1)))
        ss.append(jnp.transpose(den, (0, 2, 1)))
        nums.append(num)
    return _combine_by_denominator(ms, ss, nums)


def _mlstm(q, k, v, i_pre, f_pre, c0, n0, m0):
    n, s, h, dk = q.shape
    f32 = jnp.float32
    ln = math.gcd(s, C_CHUNK)
    nc = s // ln

    def chunks(t):
        t = t.astype(f32).reshape((n, nc, ln) + t.shape[2:])
        return jnp.swapaxes(jnp.moveaxis(t, 1, 0), 2, 3)

    qs = chunks(q)
    ks = chunks(k) * (dk ** -0.5)
    vs = chunks(v)
    ig = chunks(i_pre)
    lf = jax.nn.log_sigmoid(chunks(f_pre))
    causal = jnp.tril(jnp.ones((ln, ln), dtype=bool))

    def step(carry, xs):
        c, nv, m = carry
        qc, kc, vc, ic, lfc = xs
        b = jnp.cumsum(lfc, axis=-1)
        dmat = jnp.where(causal, b[..., :, None] - b[..., None, :] + ic[..., None, :], NEG_INF)
        inter = b + m[..., None]
        mt = jnp.maximum(inter, jnp.max(dmat, axis=-1))
        wmat = jnp.exp(dmat - mt[..., None]) * jnp.einsum('nhtk,nhsk->nhts', qc, kc)
        w_inter = jnp.exp(inter - mt)
        num = jnp.einsum('nhts,nhsv->nhtv', wmat, vc) + w_inter[..., None] * jnp.einsum('nhvk,nhtk->nhtv', c, qc)
        den = jnp.sum(wmat, axis=-1) + w_inter * jnp.einsum('nhk,nhtk->nht', nv, qc)
        hc = num / jnp.maximum(jnp.abs(den), jnp.exp(-mt))[..., None]
        m_new = mt[..., -1]
        w_end = jnp.exp(b[..., -1:] - b + ic - m_new[..., None])
        decay = jnp.exp(b[..., -1] + m - m_new)
        c_new = decay[..., None, None] * c + jnp.einsum('nhs,nhsv,nhsk->nhvk', w_end, vc, kc)
        n_new = decay[..., None] * nv + jnp.einsum('nhs,nhsk->nhk', w_end, kc)
        return (c_new, n_new, m_new), hc

    (cf, nf, mf), hs = lax.scan(step, (c0.astype(f32), n0.astype(f32), m0.astype(f32)), (qs, ks, vs, ig, lf))
    hs = jnp.moveaxis(jnp.swapaxes(hs, 2, 3), 0, 1).reshape(n, s, h, v.shape[-1])
    return hs, (cf, nf, mf)


def _layer(x, swa_past, mlstm0, norm1_g, w_in, ws_a, bs_a, norm_va_g, qn_g, kn_g, i_b, f_b, hn_c_g,
           w_branch_a, w_branch_b, w_branch_c, w_out, norm2_g, w_ff1, w_ff2):
    n, s, _ = x.shape
    proj = _rms_norm(x, norm1_g) @ w_in
    ua, va, qb, kb, vb, qc, kc, vc, oc, ic, fc, gates = jnp.split(proj, SPLIT_POINTS, axis=-1)
    ua = jax.nn.gelu(ua).reshape(n, s, A_GROUPS, A_GROUP_DIM)
    va = _rms_norm(jax.nn.gelu(va), norm_va_g)
    a_out = _chunk_spatial_gate(ua, va.reshape(n, s, A_GROUPS, A_GROUP_DIM), ws_a, bs_a).reshape(n, s, A_WIDTH)
    qh = _rms_norm(qb.reshape(n, s, B_HEADS, B_HEAD_DIM), qn_g) * (B_HEAD_DIM ** -0.5)
    kh = _rms_norm(kb.reshape(n, s, B_HEADS, B_HEAD_DIM), kn_g)
    vh = vb.reshape(n, s, B_HEADS, B_HEAD_DIM)
    if swa_past is None:
        o = _dilated_attn_prompt(qh, kh, vh)
        keep = min(B_WIN_MAX, s)
        kv_rows = (kh[:, s - keep:], vh[:, s - keep:])
    else:
        k_all = jnp.concatenate([swa_past[0], kh], axis=1)
        v_all = jnp.concatenate([swa_past[1], vh], axis=1)
        o = _dilated_attn_sample(qh, k_all, v_all)
        kv_rows = (kh, vh)
    b_out = o.astype(x.dtype).reshape(n, s, B_WIDTH)
    hc, mstate = _mlstm(qc.reshape(n, s, C_HEADS, C_QK_DIM), kc.reshape(n, s, C_HEADS, C_QK_DIM),
                        vc.reshape(n, s, C_HEADS, C_V_DIM), ic + i_b, fc + f_b, *mlstm0)
    hn = _rms_norm(hc.astype(x.dtype), hn_c_g.reshape(C_HEADS, C_V_DIM))
    c_out = (jax.nn.sigmoid(oc.reshape(n, s, C_HEADS, C_V_DIM)) * hn).reshape(n, s, C_V_WIDTH)
    g_a, g_b, g_c = jnp.split(jax.nn.sigmoid(gates), N_BRANCH, axis=-1)
    merged = g_a * (a_out @ w_branch_a) + g_b * (b_out @ w_branch_b) + g_c * (c_out @ w_branch_c)
    x = x + merged @ w_out
    ff = jax.nn.relu(_rms_norm(x, norm2_g) @ w_ff1)
    x = x + (ff * ff) @ w_ff2
    return x, kv_rows, mstate, va


def setup_inputs(seed: int = 0) -> dict:
    key = jax.random.key(seed)
    ks = jax.random.split(key, 26)
    swa_buf = min(B_WIN_MAX, PAST_LEN)

    def nrm(k, shape, scale):
        return jax.random.normal(k, shape, jnp.float32) * scale

    def gain(k, shape):
        return 1.0 + nrm(k, shape, 0.02)

    return {
        'x_prompt': nrm(ks[0], (BATCH, SEQ, D_MODEL), 1.0),
        'x_sample': nrm(ks[1], (DEC_BATCH, DEC_SEQ, D_MODEL), 1.0),
        'cache_swa_k': nrm(ks[2], (DEPTH, DEC_BATCH, swa_buf, B_HEADS, B_HEAD_DIM), 1.0),
        'cache_swa_v': nrm(ks[3], (DEPTH, DEC_BATCH, swa_buf, B_HEADS, B_HEAD_DIM), 1.0),
        'state_mlstm_C': nrm(ks[4], (DEPTH, DEC_BATCH, C_HEADS, C_V_DIM, C_QK_DIM), 0.1),
        'state_mlstm_n': nrm(ks[5], (DEPTH, DEC_BATCH, C_HEADS, C_QK_DIM), 0.1),
        'state_mlstm_m': nrm(ks[6], (DEPTH, DEC_BATCH, C_HEADS), 1.0),
        'norm1_g': gain(ks[7], (DEPTH, D_MODEL)),
        'w_in': nrm(ks[8], (DEPTH, D_MODEL, D_IN), D_MODEL ** -0.5),
        'ws_a': nrm(ks[9], (DEPTH, A_GROUPS, A_CHUNK, A_CHUNK), A_CHUNK ** -0.5),
        'bs_a': 1.0 + nrm(ks[10], (DEPTH, A_GROUPS, A_CHUNK), 0.1),
        'norm_va_g': gain(ks[11], (DEPTH, A_WIDTH)),
        'qn_g': gain(ks[12], (DEPTH, B_HEAD_DIM)),
        'kn_g': gain(ks[13], (DEPTH, B_HEAD_DIM)),
        'i_b': nrm(ks[14], (DEPTH, C_HEADS), 0.1),
        'f_b': F_BIAS_INIT + nrm(ks[15], (DEPTH, C_HEADS), 0.1),
        'hn_c_g': gain(ks[16], (DEPTH, C_V_WIDTH)),
        'w_branch_a': nrm(ks[17], (DEPTH, A_WIDTH, D_MODEL), A_WIDTH ** -0.5),
        'w_branch_b': nrm(ks[18], (DEPTH, B_WIDTH, D_MODEL), B_WIDTH ** -0.5),
        'w_branch_c': nrm(ks[19], (DEPTH, C_V_WIDTH, D_MODEL), C_V_WIDTH ** -0.5),
        'w_out': nrm(ks[20], (DEPTH, D_MODEL, D_MODEL), D_MODEL ** -0.5),
        'norm2_g': gain(ks[21], (DEPTH, D_MODEL)),
        'w_ff1': nrm(ks[22], (DEPTH, D_MODEL, D_FF), D_MODEL ** -0.5),
        'w_ff2': nrm(ks[23], (DEPTH, D_FF, D_MODEL), D_FF ** -0.5),
    }


def reference(x_prompt, x_sample, cache_swa_k, cache_swa_v, state_mlstm_C, state_mlstm_n, state_mlstm_m,
              norm1_g, w_in, ws_a, bs_a, norm_va_g, qn_g, kn_g, i_b, f_b, hn_c_g,
              w_branch_a, w_branch_b, w_branch_c, w_out, norm2_g, w_ff1, w_ff2):
    f32 = jnp.float32
    nbp = x_prompt.shape[0]
    xp, xs = x_prompt, x_sample
    kp_l, vp_l, cp_l, np_l, mp_l = [], [], [], [], []
    ks_l, vs_l, cs_l, ns_l, msl, va_l = [], [], [], [], [], []
    for l in range(DEPTH):
        lw = (norm1_g[l], w_in[l], ws_a[l], bs_a[l], norm_va_g[l], qn_g[l], kn_g[l], i_b[l], f_b[l],
              hn_c_g[l], w_branch_a[l], w_branch_b[l], w_branch_c[l], w_out[l], norm2_g[l], w_ff1[l], w_ff2[l])
        init = (jnp.zeros((nbp, C_HEADS, C_V_DIM, C_QK_DIM), f32), jnp.zeros((nbp, C_HEADS, C_QK_DIM), f32),
                jnp.zeros((nbp, C_HEADS), f32))
        xp, (kp, vp), (cp, n_p, mp), _ = _layer(xp, None, init, *lw)
        xs, (k_s, v_s), (c_s, n_s, m_s), va_s = _layer(
            xs, (cache_swa_k[l], cache_swa_v[l]), (state_mlstm_C[l], state_mlstm_n[l], state_mlstm_m[l]), *lw)
        kp_l.append(kp); vp_l.append(vp); cp_l.append(cp); np_l.append(n_p); mp_l.append(mp)
        ks_l.append(k_s); vs_l.append(v_s); cs_l.append(c_s); ns_l.append(n_s); msl.append(m_s); va_l.append(va_s)
    return (xp, xs,
            jnp.stack(kp_l), jnp.stack(vp_l), jnp.stack(cp_l), jnp.stack(np_l), jnp.stack(mp_l),
            jnp.stack(ks_l), jnp.stack(vs_l), jnp.stack(cs_l), jnp.stack(ns_l), jnp.stack(msl),
            jnp.stack(va_l))
```

```python
import contextlib
import numpy as np
import ml_dtypes
import concourse.bass as bass
import concourse.mybir as mybir
from concourse.bass_utils import run_bass_kernel_spmd

F32 = mybir.dt.float32
BF16 = mybir.dt.bfloat16
AF = mybir.ActivationFunctionType
ALU = mybir.AluOpType
AX = mybir.AxisListType

D = 4096
S = 2048
TH = 1024
NSUB = TH // 128
DEPTH = 2
NS = 4
DIN = 28688
DFF = 16384
EPS = 1e-6
O_UA, O_VA, O_QB, O_KB, O_VB, O_QC, O_KC, O_VC, O_OC, O_IC, O_FC, O_G = (
    0, 2048, 4096, 6144, 8192, 10240, 11264, 12288, 14336, 16384, 16392, 16400)
SAME_ENGINE_SYNC = True
import os
KSTOP = int(os.environ.get('KSTOP', '99'))
KTILES = int(os.environ.get('KTILES', '9999'))


class Buf:
    __slots__ = ("w", "r")

    def __init__(self):
        self.w = {}
        self.r = {}


def bufs(n):
    return [Buf() for _ in range(n)]


class Sched:
    def __init__(self, nc, n_sp=28, n_pool=12):
        self.nc = nc
        self.eng = {"pe": nc.tensor, "act": nc.scalar, "dve": nc.vector,
                    "pool": nc.gpsimd, "sp": nc.sync}
        self.sem = {}
        for e in ("pe", "act", "dve"):
            self.sem[e] = nc.alloc_semaphore("s_" + e)
        self.cnt = {e: 0 for e in ("pe", "act", "dve")}
        self.waited = {e: {} for e in self.eng}
        self.lanes = {"sp": [], "pool": []}
        for q, n in (("sp", n_sp), ("pool", n_pool)):
            for i in range(n):
                k = "%s%d" % (q, i)
                self.sem[k] = nc.alloc_semaphore("l_" + k)
                self.lanes[q].append(k)
        self.lane_cnt = {k: 0 for q in self.lanes for k in self.lanes[q]}
        self.lane_next = {"sp": 0, "pool": 0}
        self.pending = {e: set() for e in self.eng}
        self.n_instr = 0

    def _wait(self, e, deps):
        eng = self.eng[e]
        deps = set(deps) | self.pending[e]
        self.pending[e] = set()
        w = self.waited[e]
        for (k, v) in sorted(deps):
            if k == e and (e == "pe" or not SAME_ENGINE_SYNC):
                continue
            if w.get(k, 0) >= v:
                continue
            eng.wait_ge(self.sem[k], v)
            w[k] = v

    @staticmethod
    def _deps(reads, writes):
        deps = set()
        for b in reads:
            deps.update(b.w.values())
        for b in writes:
            deps.update(b.w.values())
            deps.update(b.r.values())
        return deps

    def op(self, e, fn, reads=(), writes=(), signal=True):
        self._wait(e, self._deps(reads, writes))
        ins = fn(self.eng[e])
        self.n_instr += 1
        if signal:
            self.cnt[e] += 1
            ins.then_inc(self.sem[e], 1)
            tok = (e, self.cnt[e])
        else:
            tok = (e, self.cnt[e] + 1)
        for b in reads:
            b.r[e] = tok
        for b in writes:
            b.w = {e: tok}
            b.r = {}
        return tok

    def dma(self, q, out, in_, reads=(), writes=(), accum=False, after_barrier=False):
        lanes = self.lanes[q]
        k = lanes[self.lane_next[q] % len(lanes)]
        self.lane_next[q] += 1
        deps = self._deps(reads, writes)
        if after_barrier:
            deps |= getattr(self, "last_barrier", set())
        if self.lane_cnt[k]:
            deps.add((k, self.lane_cnt[k] * 16))
        self._wait(q, deps)
        kw = {}
        if accum:
            kw["accum_op"] = ALU.add
        ins = self.eng[q].dma_start(out=out, in_=in_, **kw)
        self.n_instr += 1
        self.lane_cnt[k] += 1
        ins.then_inc(self.sem[k], 16)
        tok = (k, self.lane_cnt[k] * 16)
        for b in reads:
            b.r[k] = tok
        for b in writes:
            b.w = {k: tok}
            b.r = {}
        return tok

    def all_tokens(self, with_pool=False):
        toks = set()
        for e in ("pe", "act", "dve"):
            if self.cnt[e]:
                toks.add((e, self.cnt[e]))
        qs = ("sp", "pool") if with_pool else ("sp",)
        for q in qs:
            for k in self.lanes[q]:
                if self.lane_cnt[k]:
                    toks.add((k, self.lane_cnt[k] * 16))
        return toks

    def barrier(self):
        toks = self.all_tokens()
        self.last_barrier = set(toks)
        for e in ("pe", "act", "dve", "sp"):
            self.pending[e] |= toks

    def finish(self):
        toks = self.all_tokens(with_pool=True)
        self._wait("sp", toks)
        self.eng["sp"].wait_ge(self.sem[self.lanes["sp"][0]], 0)


class WStream:
    def __init__(self, sc, nc, nbuf=3, depth=1):
        self.sc = sc
        self.nbuf = nbuf
        self.depth = depth
        self.t = [nc.alloc_sbuf_tensor("wb%d" % i, [128, 8192], BF16) for i in range(nbuf)]
        self.b = bufs(nbuf)
        self.plan = []
        self.issued = 0
        self.used = 0

    def _issue(self):
        src, kc, ncol, rows = self.plan[self.issued]
        i = self.issued % self.nbuf
        dst = self.t[i][0:rows, 0:kc * ncol].rearrange("p (k n) -> p k n", k=kc)
        self.sc.dma("pool", dst, src, writes=[self.b[i]])
        self.issued += 1

    def next(self, key):
        assert self.plan[self.used][-1] == key or True
        while self.issued < len(self.plan) and self.issued <= self.used + self.depth:
            self._issue()
        src, kc, ncol, rows = self.plan[self.used]
        i = self.used % self.nbuf
        self.used += 1
        view = self.t[i][0:rows, 0:kc * ncol].rearrange("p (k n) -> p k n", k=kc)
        return view, self.b[i]


def c_mult(d):
    d = np.asarray(d)
    c = ((d >= 0) & (d <= 128)).astype(np.float32)
    c += ((d >= 0) & (d <= 512) & (d % 4 == 0)).astype(np.float32)
    c += ((d >= 0) & (d <= 2048) & (d % 16 == 0)).astype(np.float32)
    return c


def host_consts():
    k = np.arange(128)[:, None, None]
    dl = np.arange(16)[None, :, None]
    q = np.arange(128)[None, None, :]
    amask = c_mult(dl * 128 + q - k).reshape(128, 16 * 128).astype(np.float32)
    kk = np.arange(128)[:, None, None]
    blk = np.arange(17)[None, :, None]
    t = np.arange(NS)[None, None, :]
    kg = blk * 128 + kk
    sm = c_mult(2048 + t - kg)
    sm = np.where((blk == 16) & (kk >= NS), 0.0, sm)
    smask = sm.reshape(128, 17 * NS).astype(np.float32)
    s_ = np.arange(128)[:, None]
    t_ = np.arange(128)[None, :]
    caus = (s_ <= t_).astype(np.float32)
    keep = np.ones((8, 1024), np.float32)
    keep[:, ::128] = 0.0
    sel = np.zeros((8, 8, 128), np.float32)
    for h in range(8):
        sel[h, h, :] = 1.0
    return dict(c_ident=np.eye(128, dtype=np.float32), c_amask=amask, c_smask=smask,
                c_caus=caus, c_keep=keep, c_sel=sel.reshape(8, 1024))


def build():
    nc = bass.Bass("TRN2", target_bir_lowering=False)
    sc = Sched(nc)
    L = DEPTH

    def din(name, shape, dt=F32):
        return nc.dram_tensor(name, list(shape), dt, kind="ExternalInput").ap()

    def dout(name, shape, dt=F32):
        return nc.dram_tensor(name, list(shape), dt, kind="ExternalOutput").ap()

    def dscr(name, shape, dt=BF16):
        return nc.dram_tensor(name, list(shape), dt, kind="Internal").ap()

    def sb(name, shape, dt):
        return nc.alloc_sbuf_tensor(name, list(shape), dt)

    xT_in = din("xT", [D, S])
    xsT_in = din("xsT", [D, NS])
    SMALL = KSTOP <= 5
    L1W = 1 if KSTOP <= 10 else L
    ck_in = din("ck", [L, 2048, 2048])
    cv_in = din("cv", [L, 2048, 2048])
    st_in = din("st_in", [L, 8, 128, 257])
    m_in = din("m_in", [L, 8, 1])
    w_in = din("w_in", [L1W, D, DIN])
    w_br = [din("w_b%d" % i, [L1W, 2048, D] if KSTOP >= 6 else [1, 1, 1]) for i in range(3)]
    w_out = din("w_out", [L1W, D, D] if KSTOP >= 6 else [1, 1, 1])
    w_ff1 = din("w_ff1", [L1W, D, DFF] if KSTOP >= 7 else [1, 1, 1])
    w_ff2 = din("w_ff2", [L1W, DFF, D] if KSTOP >= 7 else [1, 1, 1])
    g1_in = din("g1", [L, 128, 32])
    g2_in = din("g2", [L, 128, 32])
    gva_in = din("gva", [L, 128, 2048])
    hng_in = din("hng", [L, 128, 2048])
    gq_in = din("gq", [L, 128, 1])
    gk_in = din("gk", [L, 128, 1])
    ib_in = din("ib", [L, 8, 1])
    fb_in = din("fb", [L, 8, 1])
    wsT_in = din("wsT", [L, 128, 1024])
    bT_in = din("bT", [L, 128, 1024])
    c_ident = din("c_ident", [128, 128])
    c_amask = din("c_amask", [128, 2048])
    c_smask = din("c_smask", [128, 17 * NS])
    c_caus = din("c_caus", [128, 128])
    c_keep = din("c_keep", [8, 1024])
    c_sel = din("c_sel", [8, 1024])

    yT = dout("yT", [D, S])
    ysT = dout("ysT", [D, NS])
    kT_out = dout("kT_out", [L, 2048, S])
    v_out = dout("v_out", [L, S, 2048])
    st_out = dout("st_out", [L, 8, 128, 257])
    m_out = dout("m_out", [L, 8, 1])
    ks_out = dout("ks_out", [L, 2048, NS])
    vs_out = dout("vs_out", [L, NS, 2048])
    sts_out = dout("sts_out", [L, 8, 128, 257])
    ms_out = dout("ms_out", [L, 8, 1])
    va_out = dout("va_out", [L, NS, 2048])
    KDBG = int(os.environ.get("KDBG", "0"))
    dbg = dout("dbg", [128, 1024]) if KDBG else None

    uT_d = dscr("uT_d", [2048, S])
    vaG_d = dscr("vaG_d", [S, 2048])
    qT_d = dscr("qT_d", [2048, S])
    kT_d = dscr("kT_d", [2048, S])
    V_d = dscr("V_d", [S, 2048])
    qcT_d = dscr("qcT_d", [1024, S])
    kcT_d = dscr("kcT_d", [1024, S])
    kc_d = dscr("kc_d", [S, 1024])
    vc_d = dscr("vc_d", [S, 2048])
    ogT_d = dscr("ogT_d", [2048, S])
    gT_d = dscr("gT_d", [3 * D, S])
    brT_d = [dscr("brT%d_d" % i, [2048, S]) for i in range(3)]
    vaGs_d = dscr("vaGs_d", [NS, 2048])
    Vs_d = dscr("Vs_d", [NS, 2048])
    kcs_d = dscr("kcs_d", [NS, 1024])
    vcs_d = dscr("vcs_d", [NS, 2048])

    yT_b = [[Buf() for _ in range(2)] for _ in range(32)]
    scr_b = {}

    def sbuf_of(key):
        if key not in scr_b:
            scr_b[key] = Buf()
        return scr_b[key]

    ws = WStream(sc, nc)
    ps = [nc.alloc_psum_tensor("ps%d" % i, [128, 512], F32) for i in range(8)]
    pb = bufs(8)
    ident_f = sb("ident_f", [128, 128], F32)
    ident_b = sb("ident_b", [128, 128], BF16)
    ones_b = sb("ones_b", [128, 128], BF16)
    amask = sb("amask", [128, 2048], BF16)
    smask = sb("smask", [128, 17 * NS], BF16)
    caus_f = sb("caus_f", [128, 128], F32)
    tmp_f = sb("tmp_f", [128, 512], F32)
    cb = Buf()
    Cst = sb("Cst", [128, 8 * 257], F32)
    Cbf = sb("Cbf", [128, 8 * 257], BF16)
    Csts = sb("Csts", [128, 8 * 257], F32)
    Cbfs = sb("Cbfs", [128, 8 * 257], BF16)
    Cst_b, Cbf_b, Csts_b, Cbfs_b = bufs(8), bufs(8), bufs(8), bufs(8)
    mst = sb("mst", [8, 1], F32)
    msts = sb("msts", [8, 1], F32)
    mst_b, msts_b = Buf(), Buf()
    xsT = sb("xsT_sb", [128, 32 * NS], F32)
    xsT_b = Buf()
    xnTs = sb("xnTs", [128, 32 * NS], BF16)
    xnTs_b = Buf()

    tb0 = Buf()
    sc.dma("sp", ident_f[:], c_ident[:, :], writes=[cb])
    sc.dma("sp", caus_f[:], c_caus[:, :], writes=[cb])
    for pc in range(4):
        sc.dma("sp", tmp_f[:], c_amask[:, pc * 512:(pc + 1) * 512], writes=[tb0])
        sc.op("dve", lambda e, pc=pc: e.tensor_copy(out=amask[:, pc * 512:(pc + 1) * 512], in_=tmp_f[:]),
              reads=[tb0], writes=[cb])
    sc.dma("sp", tmp_f[:, 0:17 * NS], c_smask[:, :], writes=[tb0])
    sc.op("dve", lambda e: e.tensor_copy(out=smask[:], in_=tmp_f[:, 0:17 * NS]), reads=[tb0], writes=[cb])
    sc.op("dve", lambda e: e.tensor_copy(out=ident_b[:], in_=ident_f[:]), reads=[cb], writes=[cb])
    sc.op("dve", lambda e: e.memset(ones_b[:], 1.0), writes=[cb])
    for fb in range(32):
        sc.dma("sp", yT[fb * 128:(fb + 1) * 128, :], xT_in[fb * 128:(fb + 1) * 128, :],
               writes=[yT_b[fb][0], yT_b[fb][1]])
    sc.dma("sp", xsT[:].rearrange("p (k t) -> p k t", k=32),
           xsT_in.rearrange("(k p) t -> p k t", p=128), writes=[xsT_b])

    state = dict(nc=nc, sc=sc, ws=ws, ps=ps, pb=pb)
    g = dict(locals())
    for l in range(L if KSTOP > 10 else (1 if KSTOP > 0 else 0)):
        emit_layer(g, l)
    sc.dma("sp", ysT.rearrange("(k p) t -> p k t", p=128),
           xsT[:].rearrange("p (k t) -> p k t", k=32), reads=[xsT_b])
    sc.finish()
    return nc


class NS_:
    pass


def win_tiles():
    T = []

    def add(sec, o, n, mode, step=256):
        for c in range(0, n, step):
            T.append((sec, o + c, min(step, n - c), mode, c))
    add("uA", O_UA, 2048, "fm")
    add("vA", O_VA, 2048, "tm")
    add("qB", O_QB, 2048, "fm")
    add("kB", O_KB, 2048, "fm")
    add("vB", O_VB, 2048, "tm")
    add("qC", O_QC, 1024, "fm")
    add("kC", O_KC, 1024, "fmtm")
    add("vC", O_VC, 2048, "tm")
    add("oC", O_OC, 2048, "fm")
    add("if", O_IC, 16, "if")
    add("g", O_G, 3 * D, "fm")
    return T


def emit_layer(g, l):
    G = NS_()
    G.__dict__.update(g)
    nc, sc, ws, ps, pb = G.nc, G.sc, G.ws, G.ps, G.pb
    G.bank_i = 0

    def sbt(es, name, shape, dt):
        return es.enter_context(nc.sbuf_tensor("%s_l%d_%d" % (name, l, sc.n_instr), list(shape), dt))

    def next_banks(n):
        i = G.bank_i
        G.bank_i += 1
        if n == 3:
            base = (i % 2) * 3
            return [base, base + 1, base + 2]
        return [i % 6]

    with contextlib.ExitStack() as LS:
        g1 = sbt(LS, "g1", [128, 32], F32)
        g2 = sbt(LS, "g2", [128, 32], F32)
        gq = sbt(LS, "gq", [128, 1], F32)
        gk = sbt(LS, "gk", [128, 1], F32)
        ibf = sbt(LS, "ibf", [8, 2], F32)
        igT = sbt(LS, "igT", [8, 1024], F32)
        lfT = sbt(LS, "lfT", [8, 1024], F32)
        igTs = sbt(LS, "igTs", [8, NS], F32)
        lfTs = sbt(LS, "lfTs", [8, NS], F32)
        ssv = sbt(LS, "ssv", [128, 8 * 8], F32)
        ssvs = sbt(LS, "ssvs", [128, 8], F32)
        uTs = sbt(LS, "uTs", [128, 16 * NS], BF16)
        qTs = sbt(LS, "qTs", [128, 16 * NS], BF16)
        kTs = sbt(LS, "kTs", [128, 16 * NS], BF16)
        qcTs = sbt(LS, "qcTs", [128, 8 * NS], BF16)
        kcTs = sbt(LS, "kcTs", [128, 8 * NS], BF16)
        ogTs = sbt(LS, "ogTs", [128, 16 * NS], BF16)
        gTs = sbt(LS, "gTs", [128, 96 * NS], BF16)
        brTs = [sbt(LS, "brTs%d" % i, [128, 16 * NS], BF16) for i in range(3)]
        lp = Buf()
        gate_b, gates_b, ssv_b, smp_b = Buf(), Buf(), Buf(), Buf()
        brTs_b = bufs(3)
        sc.dma("sp", g1[:], G.g1_in[l], writes=[lp])
        sc.dma("sp", g2[:], G.g2_in[l], writes=[lp])
        sc.dma("sp", gq[:], G.gq_in[l], writes=[lp])
        sc.dma("sp", gk[:], G.gk_in[l], writes=[lp])
        sc.dma("sp", ibf[:, 0:1], G.ib_in[l], writes=[lp])
        sc.dma("sp", ibf[:, 1:2], G.fb_in[l], writes=[lp])
        sc.op("dve", lambda e: e.tensor_scalar(out=gk[:], in0=gk[:], scalar1=float(np.sqrt(128.0)),
                                               scalar2=None, op0=ALU.mult), reads=[lp], writes=[lp])
        G.l = l
        G.lp = lp
        loc = dict(locals())
        for h in range(2 if KSTOP >= 8 else 1):
            emit_half(G, loc, l, h)
        sc.barrier()


def emit_half(G, loc, l, h):
    H = NS_()
    H.__dict__.update(loc)
    nc, sc, ws, ps, pb = G.nc, G.sc, G.ws, G.ps, G.pb
    tok0 = h * TH
    do_s = (h == 0)
    sbt = H.sbt
    next_banks = H.next_banks
    ident_b, ones_b = G.ident_b, G.ones_b
    cb = G.cb

    def kview(ap2d, k):
        return ap2d.rearrange("p (k t) -> p k t", k=k)

    def rmsnorm(es, gt, outT, outT_b, outTs):
        with contextlib.ExitStack() as st:
            stg = [sbt(st, "nstg%d" % i, [128, 8 * 512], F32) for i in range(2)]
            stg_b = bufs(2)
            sq = [sbt(st, "nsq%d" % i, [128, 512], BF16) for i in range(2)]
            sq_b = bufs(2)
            rs = sbt(st, "nrs", [128, 512], F32)
            rs_b = Buf()
            ld = 0
            for blk in range(2):
                t0 = tok0 + blk * 512
                for pas in range(2):
                    for grp in range(4):
                        i = ld % 2
                        ld += 1
                        src = G.yT[grp * 1024:(grp + 1) * 1024, t0:t0 + 512].rearrange("(k p) t -> p k t", p=128)
                        sc.dma("sp", kview(stg[i][:], 8), src,
                               reads=[G.yT_b[grp * 8 + j][h] for j in range(8)], writes=[stg_b[i]])
                        for j in range(8):
                            kc = grp * 8 + j
                            xin = stg[i][:, j * 512:(j + 1) * 512]
                            if pas == 0:
                                s_ = kc % 2
                                sc.op("act", lambda e, o=sq[s_], x=xin: e.activation(out=o[:], in_=x, func=AF.Square),
                                      reads=[stg_b[i]], writes=[sq_b[s_]])
                                sc.op("pe", lambda e, o=sq[s_], kc=kc: e.matmul(ps[6][:], ones_b[:], o[:],
                                                                                  start=(kc == 0), stop=(kc == 31)),
                                      reads=[sq_b[s_], cb], writes=[pb[6]])
                            else:
                                sc.op("dve", lambda e, x=xin, kc=kc, blk=blk: e.scalar_tensor_tensor(
                                    out=outT[:, kc * 1024 + blk * 512: kc * 1024 + (blk + 1) * 512], in0=x,
                                    scalar=gt[:, kc:kc + 1], in1=rs[:], op0=ALU.mult, op1=ALU.mult),
                                    reads=[stg_b[i], rs_b, G.lp], writes=[outT_b[blk]])
                    if pas == 0:
                        sc.op("act", lambda e: e.activation(out=rs[:], in_=ps[6][:], func=AF.Sqrt,
                                                            bias=EPS, scale=1.0 / D),
                              reads=[pb[6]], writes=[rs_b])
                        sc.op("dve", lambda e: e.reciprocal(out=rs[:], in_=rs[:]), reads=[rs_b], writes=[rs_b])
            if do_s:
                sqs = sbt(st, "nsqs", [128, 32 * NS], BF16)
                rss = sbt(st, "nrss", [128, NS], F32)
                tmps = sbt(st, "ntmps", [128, 32 * NS], F32)
                b1 = Buf()
                sc.op("act", lambda e: e.activation(out=sqs[:], in_=G.xsT[:], func=AF.Square),
                      reads=[G.xsT_b], writes=[b1])
                for kc in range(32):
                    sc.op("pe", lambda e, kc=kc: e.matmul(ps[7][:, 0:NS], ones_b[:], sqs[:, kc * NS:(kc + 1) * NS],
                                                          start=(kc == 0), stop=(kc == 31)),
                          reads=[b1, cb], writes=[pb[7]], signal=(kc == 31))
                sc.op("act", lambda e: e.activation(out=rss[:], in_=ps[7][:, 0:NS], func=AF.Sqrt,
                                                    bias=EPS, scale=1.0 / D), reads=[pb[7]], writes=[b1])
                sc.op("dve", lambda e: e.reciprocal(out=rss[:], in_=rss[:]), reads=[b1], writes=[b1])
                sc.op("dve", lambda e: e.tensor_tensor(
                    out=kview(tmps[:], 32), in0=kview(G.xsT[:], 32),
                    in1=gt[:, :].unsqueeze(2).broadcast_to([128, 32, NS]), op=ALU.mult),
                    reads=[G.xsT_b, G.lp, b1], writes=[b1])
                sc.op("dve", lambda e: e.tensor_tensor(
                    out=kview(outTs[:], 32), in0=kview(tmps[:], 32),
                    in1=rss[:, :].unsqueeze(1).broadcast_to([128, 32, NS]), op=ALU.mult),
                    reads=[b1], writes=[G.xnTs_b])
        sc.barrier()

    H.kview = kview
    H.rmsnorm = rmsnorm
    H.gemm = lambda *a, **k: gemm(G, H, *a, **k)
    H.tok0 = tok0
    H.do_s = do_s
    H.h = h
    stage_win(G, H)
    if KSTOP <= 2:
        return
    stage_gmlp(G, H)
    if KSTOP <= 3:
        return
    stage_attn(G, H)
    if KSTOP <= 4:
        return
    stage_mlstm(G, H)
    if G.dbg is not None and h == 0 and l == 0:
        with contextlib.ExitStack() as es_:
            dt_ = sbt(es_, "dbgt", [128, 1024], F32)
            db_ = Buf()
            sc.op("dve", lambda e: e.memset(dt_[:], 0.0), writes=[db_])
            for i_ in range(3):
                sc.op("dve", lambda e, i_=i_: e.tensor_copy(out=dt_[:, i_ * 64:(i_ + 1) * 64], in_=H.brTs[i_][:]),
                      reads=[H.brTs_b[i_]], writes=[db_])
            sc.op("dve", lambda e: e.tensor_copy(out=dt_[:, 192:192 + 96 * NS], in_=H.gTs[:]), reads=[H.smp_b], writes=[db_])
            sc.dma("sp", G.dbg[:, :], dt_[:], reads=[db_])
        sc.barrier()
    if KSTOP <= 5:
        return
    stage_merge(G, H)
    if KSTOP <= 6:
        return
    stage_ffn(G, H)


def gemm(G, H, es, tiles, wsrc_of, kcn, actT, actT_b, actTs, actTs_b, epi_fm, epi_tm, epi_if=None):
    sc, ws, ps, pb = G.sc, G.ws, G.ps, G.pb
    next_banks, do_s = H.next_banks, H.do_s
    ws.plan.extend([(wsrc_of(c0, ncol), kcn, ncol, 128) for (sec, c0, ncol, mode, crel) in tiles])
    for (sec, c0, ncol, mode, crel) in tiles:
        wv, wb = ws.next(None)
        if "fm" in mode:
            for cbk in range((ncol + 127) // 128):
                m = min(128, ncol - cbk * 128)
                bA, bB, bS = next_banks(3)
                for kc in range(kcn):
                    lhsT = wv[:, kc, cbk * 128:cbk * 128 + m]
                    last = (kc == kcn - 1)
                    for bk, blk in ((bA, 0), (bB, 1)):
                        sc.op("pe", lambda e, bk=bk, blk=blk, lhsT=lhsT, kc=kc: e.matmul(
                            ps[bk][0:m, :], lhsT, actT[:, kc * 1024 + blk * 512: kc * 1024 + (blk + 1) * 512],
                            start=(kc == 0), stop=(kc == kcn - 1)),
                            reads=[wb, actT_b[blk]], writes=[pb[bk]], signal=last)
                    if do_s:
                        sc.op("pe", lambda e, lhsT=lhsT, kc=kc: e.matmul(
                            ps[bS][0:m, 0:NS], lhsT, actTs[:, kc * NS:(kc + 1) * NS],
                            start=(kc == 0), stop=(kc == kcn - 1)),
                            reads=[wb, actTs_b], writes=[pb[bS]], signal=last)
                epi_fm(sec, crel + cbk * 128, m, bA, bB, bS)
        if "tm" in mode:
            for st_ in range(NSUB):
                (bk,) = next_banks(1)
                for kc in range(kcn):
                    sc.op("pe", lambda e, bk=bk, kc=kc, st_=st_: e.matmul(
                        ps[bk][:, 0:ncol], actT[:, kc * 1024 + st_ * 128: kc * 1024 + (st_ + 1) * 128],
                        wv[:, kc, 0:ncol], start=(kc == 0), stop=(kc == kcn - 1)),
                        reads=[wb, actT_b[st_ // 4]], writes=[pb[bk]], signal=(kc == kcn - 1))
                epi_tm(sec, crel, ncol, st_, bk, False)
            if do_s:
                (bk,) = next_banks(1)
                for kc in range(kcn):
                    sc.op("pe", lambda e, bk=bk, kc=kc: e.matmul(
                        ps[bk][0:NS, 0:ncol], actTs[:, kc * NS:(kc + 1) * NS], wv[:, kc, 0:ncol],
                        start=(kc == 0), stop=(kc == kcn - 1)),
                        reads=[wb, actTs_b], writes=[pb[bk]], signal=(kc == kcn - 1))
                epi_tm(sec, crel, ncol, 0, bk, True)
        if mode == "if":
            for which in range(2):
                bA, bB, bS = next_banks(3)
                for kc in range(kcn):
                    lhsT = wv[:, kc, which * 8:(which + 1) * 8]
                    last = (kc == kcn - 1)
                    for bk, blk in ((bA, 0), (bB, 1)):
                        sc.op("pe", lambda e, bk=bk, blk=blk, lhsT=lhsT, kc=kc: e.matmul(
                            ps[bk][0:8, :], lhsT, actT[:, kc * 1024 + blk * 512: kc * 1024 + (blk + 1) * 512],
                            start=(kc == 0), stop=(kc == kcn - 1)),
                            reads=[wb, actT_b[blk]], writes=[pb[bk]], signal=last)
                    if do_s:
                        sc.op("pe", lambda e, lhsT=lhsT, kc=kc: e.matmul(
                            ps[bS][0:8, 0:NS], lhsT, actTs[:, kc * NS:(kc + 1) * NS],
                            start=(kc == 0), stop=(kc == kcn - 1)),
                            reads=[wb, actTs_b], writes=[pb[bS]], signal=last)
                epi_if(which, bA, bB, bS)


def stage_win(G, H):
    nc, sc, ws, ps, pb = G.nc, G.sc, G.ws, G.ps, G.pb
    l, h, tok0, do_s, sbt = G.l, H.h, H.tok0, H.do_s, H.sbt
    ones_b, cb = G.ones_b, G.cb
    with contextlib.ExitStack() as es:
        xnT = sbt(es, "xnT", [128, 32 * 1024], BF16)
        xnT_b = bufs(2)
        H.rmsnorm(es, H.g1, xnT, xnT_b, G.xnTs)
        if KSTOP <= 1:
            return
        stA = [sbt(es, "stA%d" % i, [128, 1024], BF16) for i in range(3)]
        stA_b = bufs(3)
        raw = [sbt(es, "raw%d" % i, [128, 1024], F32) for i in range(2)]
        raw_b = bufs(2)
        sqh = sbt(es, "sqh", [128, 1024], BF16)
        sqh_b = Buf()
        rsh = sbt(es, "rsh", [128, 1024], F32)
        rsh_b = Buf()
        tmb = [sbt(es, "tmb%d" % i, [128, 256], BF16) for i in range(3)]
        tmb_b = bufs(3)
        tmf = [sbt(es, "tmf%d" % i, [128, 256], F32) for i in range(2)]
        tmf_b = bufs(2)
        junk = sbt(es, "junk", [128, 256], BF16)
        junk_b = Buf()
        sm32 = sbt(es, "sm32", [128, 8 * NS], F32)
        sm_b = Buf()
        nfb = sbt(es, "nfb", [8, 1], F32)
        ift = sbt(es, "ift", [8, 512], F32)
        ift_b = Buf()
        sc.op("dve", lambda e: e.tensor_scalar(out=nfb[:], in0=H.ibf[:, 1:2], scalar1=-1.0, scalar2=None,
                                               op0=ALU.mult), reads=[G.lp], writes=[ift_b])
        cnt = dict(a=0, r=0, tb=0, tf=0)

        def fm_store(func, bA, bB, dst, scale=1.0):
            i = cnt["a"] % 3
            cnt["a"] += 1
            for bk, blk in ((bA, 0), (bB, 1)):
                sc.op("act", lambda e, bk=bk, blk=blk: e.activation(
                    out=stA[i][0:128, blk * 512:(blk + 1) * 512], in_=ps[bk][:, :], func=func, scale=scale),
                    reads=[pb[bk]], writes=[stA_b[i]])
            sc.dma("sp", dst, stA[i][:, :], reads=[stA_b[i]], writes=[])

        def headnorm(is_k, f, bA, bB, bS):
            r = cnt["r"] % 2
            cnt["r"] += 1
            gsc = H.gk if is_k else H.gq
            for bk, blk in ((bA, 0), (bB, 1)):
                sl = slice(blk * 512, (blk + 1) * 512)
                sc.op("act", lambda e, bk=bk, sl=sl: e.activation(out=sqh[:, sl], in_=ps[bk][:, :], func=AF.Square),
                      writes=[sqh_b, pb[bk]])
                sc.op("dve", lambda e, bk=bk, sl=sl: e.tensor_copy(out=raw[r][:, sl], in_=ps[bk][:, :]),
                      writes=[raw_b[r], pb[bk]])
            for blk in range(2):
                sl = slice(blk * 512, (blk + 1) * 512)
                sc.op("pe", lambda e, blk=blk, sl=sl: e.matmul(ps[6 + blk][:, :], ones_b[:], sqh[:, sl],
                                                             start=True, stop=True),
                      reads=[sqh_b, cb], writes=[pb[6 + blk]])
                sc.op("act", lambda e, blk=blk, sl=sl: e.activation(out=rsh[:, sl], in_=ps[6 + blk][:, :],
                                                                  func=AF.Sqrt, bias=128.0 * EPS, scale=1.0),
                      reads=[pb[6 + blk]], writes=[rsh_b])
            sc.op("dve", lambda e: e.reciprocal(out=rsh[:], in_=rsh[:]), reads=[rsh_b], writes=[rsh_b])
            i = cnt["a"] % 3
            cnt["a"] += 1
            if is_k:
                sc.op("dve", lambda e: e.scalar_tensor_tensor(out=raw[r][:], in0=raw[r][:], scalar=gsc[:, 0:1],
                                                              in1=rsh[:], op0=ALU.mult, op1=ALU.mult),
                      reads=[raw_b[r], rsh_b, G.lp], writes=[raw_b[r]])
                sc.dma("sp", G.kT_out[l, f:f + 128, tok0:tok0 + TH], raw[r][:], reads=[raw_b[r]])
                sc.op("act", lambda e: e.activation(out=stA[i][:], in_=raw[r][:], func=AF.Copy),
                      reads=[raw_b[r]], writes=[stA_b[i]])
                sc.dma("sp", G.kT_d[f:f + 128, tok0:tok0 + TH], stA[i][:], reads=[stA_b[i]])
            else:
                sc.op("dve", lambda e: e.scalar_tensor_tensor(out=stA[i][:], in0=raw[r][:], scalar=gsc[:, 0:1],
                                                              in1=rsh[:], op0=ALU.mult, op1=ALU.mult),
                      reads=[raw_b[r], rsh_b, G.lp], writes=[stA_b[i]])
                sc.dma("sp", G.qT_d[f:f + 128, tok0:tok0 + TH], stA[i][:], reads=[stA_b[i]])
            if do_s:
                hd = f // 128
                sc.op("act", lambda e: e.activation(out=sqh[:, 0:NS], in_=ps[bS][:, 0:NS], func=AF.Square),
                      writes=[sqh_b, pb[bS]])
                sc.op("dve", lambda e: e.tensor_copy(out=sm32[:, 0:NS], in_=ps[bS][:, 0:NS]),
                      writes=[sm_b, pb[bS]])
                sc.op("pe", lambda e: e.matmul(ps[6][:, 0:NS], ones_b[:], sqh[:, 0:NS], start=True, stop=True),
                      reads=[sqh_b, cb], writes=[pb[6]])
                sc.op("act", lambda e: e.activation(out=sm32[:, NS:2 * NS], in_=ps[6][:, 0:NS], func=AF.Sqrt,
                                                    bias=128.0 * EPS, scale=1.0), reads=[pb[6]], writes=[sm_b])
                sc.op("dve", lambda e: e.reciprocal(out=sm32[:, NS:2 * NS], in_=sm32[:, NS:2 * NS]),
                      reads=[sm_b], writes=[sm_b])
                sc.op("dve", lambda e: e.scalar_tensor_tensor(out=sm32[:, 2 * NS:3 * NS], in0=sm32[:, 0:NS],
                                                              scalar=gsc[:, 0:1], in1=sm32[:, NS:2 * NS],
                                                              op0=ALU.mult, op1=ALU.mult),
                      reads=[sm_b, G.lp], writes=[sm_b])
                dstT = H.kTs if is_k else H.qTs
                sc.op("act", lambda e: e.activation(out=dstT[:, hd * NS:(hd + 1) * NS], in_=sm32[:, 2 * NS:3 * NS],
                                                    func=AF.Copy), reads=[sm_b], writes=[H.smp_b])
                if is_k:
                    sc.dma("sp", G.ks_out[l, f:f + 128, :], sm32[:, 2 * NS:3 * NS], reads=[sm_b])

        def epi_fm(sec, f, m, bA, bB, bS):
            fb = f // 128
            if sec == "uA":
                fm_store(AF.Gelu, bA, bB, G.uT_d[f:f + 128, tok0:tok0 + TH])
                sfun, sdst, sscale = AF.Gelu, H.uTs, 1.0
            elif sec == "qB":
                headnorm(False, f, bA, bB, bS)
                return
            elif sec == "kB":
                headnorm(True, f, bA, bB, bS)
                return
            elif sec == "qC":
                fm_store(AF.Copy, bA, bB, G.qcT_d[f:f + 128, tok0:tok0 + TH])
                sfun, sdst, sscale = AF.Copy, H.qcTs, 1.0
            elif sec == "kC":
                fm_store(AF.Copy, bA, bB, G.kcT_d[f:f + 128, tok0:tok0 + TH], scale=128.0 ** -0.5)
                sfun, sdst, sscale = AF.Copy, H.kcTs, 128.0 ** -0.5
            elif sec == "oC":
                fm_store(AF.Sigmoid, bA, bB, G.ogT_d[f:f + 128, tok0:tok0 + TH])
                sfun, sdst, sscale = AF.Sigmoid, H.ogTs, 1.0
            else:
                fm_store(AF.Sigmoid, bA, bB, G.gT_d[f:f + 128, tok0:tok0 + TH])
                sfun, sdst, sscale = AF.Sigmoid, H.gTs, 1.0
            if do_s:
                sc.op("act", lambda e: e.activation(out=sdst[:, fb * NS:(fb + 1) * NS], in_=ps[bS][:, 0:NS],
                                                    func=sfun, scale=sscale), reads=[pb[bS]], writes=[H.smp_b])

        def epi_tm(sec, crel, ncol, st_, bk, smp):
            R = NS if smp else 128
            r0 = 0 if smp else tok0 + st_ * 128
            i = cnt["tb"] % 3
            cnt["tb"] += 1
            if sec == "vA":
                sc.op("act", lambda e: e.activation(out=tmb[i][0:R, :], in_=ps[bk][0:R, 0:256], func=AF.Gelu),
                      reads=[pb[bk]], writes=[tmb_b[i]])
                acc = (H.ssvs[0:R, crel // 256: crel // 256 + 1] if smp
                       else H.ssv[:, st_ * 8 + crel // 256: st_ * 8 + crel // 256 + 1])
                sc.op("act", lambda e: e.activation(out=junk[0:R, :], in_=tmb[i][0:R, :], func=AF.Square,
                                                    accum_out=acc),
                      reads=[tmb_b[i]], writes=[junk_b, H.ssv_b])
                dst = G.vaGs_d if smp else G.vaG_d
                sc.dma("sp", dst[r0:r0 + R, crel:crel + 256], tmb[i][0:R, :], reads=[tmb_b[i]])
            elif sec == "vB":
                j = cnt["tf"] % 2
                cnt["tf"] += 1
                sc.op("dve", lambda e: e.tensor_copy(out=tmf[j][0:R, :], in_=ps[bk][0:R, 0:256]),
                      writes=[tmf_b[j], pb[bk]])
                sc.op("act", lambda e: e.activation(out=tmb[i][0:R, :], in_=tmf[j][0:R, :], func=AF.Copy),
                      reads=[tmf_b[j]], writes=[tmb_b[i]])
                d32 = G.vs_out[l] if smp else G.v_out[l]
                d16 = G.Vs_d if smp else G.V_d
                sc.dma("sp", d32[r0:r0 + R, crel:crel + 256], tmf[j][0:R, :], reads=[tmf_b[j]])
                sc.dma("sp", d16[r0:r0 + R, crel:crel + 256], tmb[i][0:R, :], reads=[tmb_b[i]])
            else:
                scale = 128.0 ** -0.5 if sec == "kC" else 1.0
                sc.op("act", lambda e: e.activation(out=tmb[i][0:R, :], in_=ps[bk][0:R, 0:256], func=AF.Copy,
                                                    scale=scale), reads=[pb[bk]], writes=[tmb_b[i]])
                if sec == "kC":
                    dst = G.kcs_d if smp else G.kc_d
                else:
                    dst = G.vcs_d if smp else G.vc_d
                sc.dma("sp", dst[r0:r0 + R, crel:crel + 256], tmb[i][0:R, :], reads=[tmb_b[i]])

        def epi_if(which, bA, bB, bS):
            srcs = [(bA, H.igT if which == 0 else H.lfT, slice(0, 512), 512),
                    (bB, H.igT if which == 0 else H.lfT, slice(512, 1024), 512)]
            if do_s:
                srcs.append((bS, H.igTs if which == 0 else H.lfTs, slice(0, NS), NS))
            for bk, dst, sl, n in srcs:
                if which == 0:
                    sc.op("act", lambda e, bk=bk, dst=dst, sl=sl, n=n: e.activation(
                        out=dst[:, sl], in_=ps[bk][0:8, 0:n], func=AF.Identity, bias=H.ibf[:, 0:1]),
                        reads=[pb[bk], G.lp], writes=[H.gate_b])
                else:
                    sc.op("act", lambda e, bk=bk, n=n: e.activation(
                        out=ift[:, 0:n], in_=ps[bk][0:8, 0:n], func=AF.Exp, bias=nfb[:, 0:1], scale=-1.0),
                        reads=[pb[bk], ift_b], writes=[ift_b])
                    sc.op("act", lambda e, n=n: e.activation(out=ift[:, 0:n], in_=ift[:, 0:n], func=AF.Ln, bias=1.0),
                          reads=[ift_b], writes=[ift_b])
                    sc.op("dve", lambda e, dst=dst, sl=sl, n=n: e.tensor_scalar(
                        out=dst[:, sl], in0=ift[:, 0:n], scalar1=-1.0, scalar2=None, op0=ALU.mult),
                        reads=[ift_b], writes=[H.gate_b])

        wl = G.w_in[l].rearrange("(k p) n -> p k n", p=128)
        H.gemm(es, win_tiles()[:KTILES], lambda c0, n: wl[:, :, c0:c0 + n], 32, xnT, xnT_b, G.xnTs, G.xnTs_b,
               epi_fm, epi_tm, epi_if)
    sc.barrier()


def stage_gmlp(G, H):
    nc, sc, ws, ps, pb = G.nc, G.sc, G.ws, G.ps, G.pb
    l, h, tok0, do_s, sbt = G.l, H.h, H.tok0, H.do_s, H.sbt
    with contextlib.ExitStack() as es:
        gva = sbt(es, "gva", [128, 2048], F32)
        wT_f = sbt(es, "wT_f", [128, 1024], F32)
        wT = sbt(es, "wT", [128, 1024], BF16)
        bT = sbt(es, "bT", [128, 1024], F32)
        rst = sbt(es, "rst", [128, 8], F32)
        cst_b = Buf()
        sc.dma("sp", gva[:], G.gva_in[l], writes=[cst_b])
        sc.dma("sp", wT_f[:], G.wsT_in[l], writes=[cst_b])
        sc.dma("sp", bT[:], G.bT_in[l], writes=[cst_b])
        for g_ in range(8):
            sc.op("dve", lambda e, g_=g_: e.tensor_tensor(out=wT[:, g_ * 128:(g_ + 1) * 128],
                                                         in0=wT_f[:, g_ * 128:(g_ + 1) * 128], in1=G.caus_f[:],
                                                         op=ALU.mult), reads=[cst_b, G.cb], writes=[cst_b])
        sc.op("dve", lambda e: e.tensor_reduce(out=rst[:], in_=H.ssv[:].rearrange("p (s c) -> p s c", c=8),
                                               axis=AX.X, op=ALU.add), reads=[H.ssv_b], writes=[cst_b])
        sc.op("act", lambda e: e.activation(out=rst[:], in_=rst[:], func=AF.Sqrt, bias=EPS, scale=1.0 / 2048),
              reads=[cst_b], writes=[cst_b])
        sc.op("dve", lambda e: e.reciprocal(out=rst[:], in_=rst[:]), reads=[cst_b], writes=[cst_b])
        va = [sbt(es, "va%d" % i, [128, 2048], BF16) for i in range(2)]
        va_b = bufs(2)
        van = [sbt(es, "van%d" % i, [128, 2048], BF16) for i in range(2)]
        van_b = bufs(2)
        uTc = [sbt(es, "uTc%d" % i, [128, 2048], BF16) for i in range(2)]
        uTc_b = bufs(2)
        ao = [sbt(es, "ao%d" % i, [128, 2048], BF16) for i in range(2)]
        ao_b = bufs(2)
        tmp = sbt(es, "gtmp", [128, 256], F32)
        tmp_b = Buf()
        for st_ in range(NSUB):
            i = st_ % 2
            t0 = tok0 + st_ * 128
            sc.dma("sp", va[i][:], G.vaG_d[t0:t0 + 128, :], writes=[va_b[i]])
            sc.dma("sp", uTc[i][:].rearrange("p (k t) -> p k t", k=16),
                   G.uT_d[:, t0:t0 + 128].rearrange("(k p) t -> p k t", p=128), writes=[uTc_b[i]])
            sc.op("dve", lambda e, i=i, st_=st_: e.scalar_tensor_tensor(
                out=van[i][:], in0=va[i][:], scalar=rst[:, st_:st_ + 1], in1=gva[:], op0=ALU.mult, op1=ALU.mult),
                reads=[va_b[i], cst_b], writes=[van_b[i]])
            for q in range(4):
                (bk,) = H.next_banks(1)
                for j in range(4):
                    fb = q * 4 + j
                    g_ = fb // 2
                    sc.op("pe", lambda e, bk=bk, j=j, fb=fb, g_=g_, i=i: e.matmul(
                        ps[bk][:, j * 128:(j + 1) * 128], van[i][:, fb * 128:(fb + 1) * 128],
                        wT[:, g_ * 128:(g_ + 1) * 128], start=True, stop=True),
                        reads=[van_b[i], cst_b], writes=[pb[bk]])
                for gg in range(2):
                    g_ = q * 2 + gg
                    fb0 = q * 4 + gg * 2
                    sc.op("dve", lambda e, bk=bk, gg=gg, g_=g_: e.tensor_tensor(
                        out=tmp[:].rearrange("p (a t) -> p a t", a=2),
                        in0=ps[bk][:, gg * 256:(gg + 1) * 256].rearrange("p (a t) -> p a t", a=2),
                        in1=bT[:, g_ * 128:(g_ + 1) * 128].unsqueeze(1).broadcast_to([128, 2, 128]), op=ALU.add),
                        reads=[pb[bk], cst_b], writes=[tmp_b])
                    sc.op("dve", lambda e, fb0=fb0, i=i: e.tensor_tensor(
                        out=ao[i][:, fb0 * 128:(fb0 + 2) * 128], in0=tmp[:], in1=uTc[i][:, fb0 * 128:(fb0 + 2) * 128],
                        op=ALU.mult), reads=[tmp_b, uTc_b[i]], writes=[ao_b[i]])
            sc.dma("sp", G.brT_d[0][:, t0:t0 + 128].rearrange("(k p) t -> p k t", p=128),
                   ao[i][:].rearrange("p (k t) -> p k t", k=16), reads=[ao_b[i]])
        if do_s:
            vas = sbt(es, "vas", [NS, 2048], BF16)
            vaf = sbt(es, "vaf", [NS, 2048], F32)
            rss = sbt(es, "grss", [NS, 1], F32)
            sb_ = Buf()
            sc.dma("sp", vas[:], G.vaGs_d[:, :], writes=[sb_])
            sc.op("dve", lambda e: e.tensor_reduce(out=rss[:], in_=H.ssvs[0:NS, 0:8], axis=AX.X, op=ALU.add),
                  reads=[H.ssv_b], writes=[sb_])
            sc.op("act", lambda e: e.activation(out=rss[:], in_=rss[:], func=AF.Sqrt, bias=EPS, scale=1.0 / 2048),
                  reads=[sb_], writes=[sb_])
            sc.op("dve", lambda e: e.reciprocal(out=rss[:], in_=rss[:]), reads=[sb_], writes=[sb_])
            sc.op("dve", lambda e: e.scalar_tensor_tensor(out=vaf[:], in0=vas[:], scalar=rss[:, 0:1], in1=gva[0:NS, :],
                                                          op0=ALU.mult, op1=ALU.mult),
                  reads=[sb_, cst_b], writes=[sb_])
            sc.dma("sp", G.va_out[l], vaf[:], reads=[sb_])
            sc.op("act", lambda e: e.activation(out=vas[:], in_=vaf[:], func=AF.Copy), reads=[sb_], writes=[sb_])
            (bk,) = H.next_banks(1)
            for fb in range(16):
                g_ = fb // 2
                sc.op("pe", lambda e, fb=fb, g_=g_: e.matmul(
                    ps[bk][:, fb * NS:(fb + 1) * NS], vas[0:NS, fb * 128:(fb + 1) * 128],
                    wT[0:NS, g_ * 128:g_ * 128 + NS], start=True, stop=True),
                    reads=[sb_, cst_b], writes=[pb[bk]])
            for fb in range(16):
                g_ = fb // 2
                sc.op("dve", lambda e, fb=fb, g_=g_: e.tensor_tensor(
                    out=tmp[:, 0:NS], in0=ps[bk][:, fb * NS:(fb + 1) * NS], in1=bT[:, g_ * 128:g_ * 128 + NS],
                    op=ALU.add), reads=[pb[bk], cst_b], writes=[tmp_b])
                sc.op("dve", lambda e, fb=fb: e.tensor_tensor(
                    out=H.brTs[0][:, fb * NS:(fb + 1) * NS], in0=tmp[:, 0:NS], in1=H.uTs[:, fb * NS:(fb + 1) * NS],
                    op=ALU.mult), reads=[tmp_b, H.smp_b], writes=[H.brTs_b[0]])
    sc.barrier()


def stage_attn(G, H):
    nc, sc, ws, ps, pb = G.nc, G.sc, G.ws, G.ps, G.pb
    l, h, tok0, do_s, sbt = G.l, H.h, H.tok0, H.do_s, H.sbt
    ones_b, ident_b, amask, smask, cb = G.ones_b, G.ident_b, G.amask, G.smask, G.cb
    with contextlib.ExitStack() as es:
        nk = tok0 + TH
        nkb = nk // 128
        kT = [sbt(es, "akT%d" % i, [128, 2048], BF16) for i in range(2)]
        V = [sbt(es, "aV%d" % i, [128, 2048], BF16) for i in range(2)]
        qT = [sbt(es, "aqT%d" % i, [128, 1024], BF16) for i in range(2)]
        in_b = bufs(2)
        pt = [sbt(es, "apt%d" % i, [128, 512], BF16) for i in range(3)]
        pt_b = bufs(3)
        ob = [sbt(es, "aob%d" % i, [128, 1024], BF16) for i in range(2)]
        ob_b = bufs(2)
        rz = sbt(es, "arz", [128, 512], F32)
        rz_b = Buf()
        it = 0
        for hd in range(16):
            i = hd % 2
            hs = slice(hd * 128, (hd + 1) * 128)
            sc.dma("sp", kT[i][:, 0:nk], G.kT_d[hs, 0:nk], writes=[in_b[i]])
            sc.dma("sp", V[i][:, 0:nkb * 128].rearrange("p (k d) -> p k d", k=nkb),
                   G.V_d[0:nk, hs].rearrange("(k p) d -> p k d", p=128), writes=[in_b[i]])
            sc.dma("sp", qT[i][:], G.qT_d[hs, tok0:tok0 + TH], writes=[in_b[i]])
            for qb in range(2):
                qs = (tok0 + qb * 512) // 128
                nj = qs + 4
                bO, bZ = (3, 4) if it % 2 == 0 else (5, 6)
                it += 1

                def smm(j):
                    c0 = max(0, j - qs) * 128
                    sc.op("pe", lambda e: e.matmul(ps[j % 3][:, c0:512], kT[i][:, j * 128:(j + 1) * 128],
                                                   qT[i][:, qb * 512 + c0:(qb + 1) * 512], start=True, stop=True),
                          reads=[in_b[i]], writes=[pb[j % 3]])
                smm(0)
                for j in range(nj):
                    if j + 1 < nj:
                        smm(j + 1)
                    i0 = max(0, j - qs)
                    c0 = i0 * 128
                    dl0 = qs + i0 - j
                    pi = j % 3
                    sc.op("act", lambda e, j=j, c0=c0, pi=pi: e.activation(out=pt[pi][:, c0:512], in_=ps[j % 3][:, c0:512],
                                                                          func=AF.Exp),
                          reads=[pb[j % 3]], writes=[pt_b[pi]])
                    sc.op("dve", lambda e, c0=c0, pi=pi, dl0=dl0: e.tensor_tensor(
                        out=pt[pi][:, c0:512], in0=pt[pi][:, c0:512], in1=amask[:, dl0 * 128: dl0 * 128 + 512 - c0],
                        op=ALU.mult), reads=[pt_b[pi], cb], writes=[pt_b[pi]])
                    last = (j == nj - 1)
                    sc.op("pe", lambda e, j=j, c0=c0, pi=pi: e.matmul(ps[bO][:, c0:512], V[i][:, j * 128:(j + 1) * 128],
                                                                     pt[pi][:, c0:512], start=(j == 0), stop=(j == nj - 1)),
                          reads=[pt_b[pi], in_b[i]], writes=[pb[bO]], signal=last)
                    sc.op("pe", lambda e, j=j, c0=c0, pi=pi: e.matmul(ps[bZ][:, c0:512], ones_b[:], pt[pi][:, c0:512],
                                                                     start=(j == 0), stop=(j == nj - 1)),
                          reads=[pt_b[pi], cb], writes=[pb[bZ]], signal=True)
                sc.op("dve", lambda e: e.reciprocal(out=rz[:], in_=ps[bZ][:, :]), reads=[pb[bZ]], writes=[rz_b])
                sc.op("dve", lambda e: e.tensor_tensor(out=ob[i][:, qb * 512:(qb + 1) * 512], in0=ps[bO][:, :], in1=rz[:],
                                                       op=ALU.mult), reads=[pb[bO], rz_b], writes=[ob_b[i]])
            sc.dma("sp", G.brT_d[1][hs, tok0:tok0 + TH], ob[i][:], reads=[ob_b[i]])
        if do_s:
            Kc = sbt(es, "sKc", [128, 16 * 512], BF16)
            Vc = sbt(es, "sVc", [128, 16 * 512], BF16)
            c_b = Buf()
            Vs_sb = sbt(es, "sVs", [NS, 2048], BF16)
            vs_b = Buf()
            KT = sbt(es, "sKT", [128, 2048], BF16)
            KT_b = Buf()
            P = sbt(es, "sP", [128, 17 * NS], BF16)
            P_b = Buf()
            rzs = sbt(es, "srz", [128, NS], F32)
            sc.dma("sp", Vs_sb[:], G.Vs_d[:, :], writes=[vs_b])
            for hg in range(4):
                sc.dma("pool", Kc[:].rearrange("p (k c) -> p k c", k=16),
                       G.ck_in[l][:, hg * 512:(hg + 1) * 512].rearrange("(k p) c -> p k c", p=128), writes=[c_b],
                       after_barrier=True)
                sc.dma("pool", Vc[:].rearrange("p (k c) -> p k c", k=16),
                       G.cv_in[l][:, hg * 512:(hg + 1) * 512].rearrange("(k p) c -> p k c", p=128), writes=[c_b],
                       after_barrier=True)
                for hh in range(4):
                    hd = hg * 4 + hh
                    for k4 in range(4):
                        bk = k4 % 3
                        for j in range(4):
                            kb = k4 * 4 + j
                            sc.op("pe", lambda e, bk=bk, j=j, kb=kb: e.matmul(
                                ps[bk][:, j * 128:(j + 1) * 128], Kc[:, kb * 512 + hh * 128: kb * 512 + (hh + 1) * 128],
                                ident_b[:], start=True, stop=True), reads=[c_b, cb], writes=[pb[bk]])
                        sc.op("act", lambda e, bk=bk, k4=k4: e.activation(out=KT[:, k4 * 512:(k4 + 1) * 512],
                                                                          in_=ps[bk][:, :], func=AF.Copy),
                              reads=[pb[bk]], writes=[KT_b])
                    qs_ = H.qTs[:, hd * NS:(hd + 1) * NS]
                    for kb in range(16):
                        sc.op("pe", lambda e, kb=kb: e.matmul(ps[3][:, kb * NS:(kb + 1) * NS], KT[:, kb * 128:(kb + 1) * 128],
                                                              qs_, start=True, stop=True),
                              reads=[KT_b, H.smp_b], writes=[pb[3]])
                    sc.op("pe", lambda e: e.matmul(ps[3][0:NS, 16 * NS:17 * NS], H.kTs[:, hd * NS:(hd + 1) * NS], qs_,
                                                   start=True, stop=True), reads=[H.smp_b], writes=[pb[3]])
                    sc.op("act", lambda e: e.activation(out=P[:, 0:16 * NS], in_=ps[3][:, 0:16 * NS], func=AF.Exp),
                          reads=[pb[3]], writes=[P_b])
                    sc.op("act", lambda e: e.activation(out=P[0:NS, 16 * NS:17 * NS], in_=ps[3][0:NS, 16 * NS:17 * NS],
                                                        func=AF.Exp), reads=[pb[3]], writes=[P_b])
                    sc.op("dve", lambda e: e.tensor_tensor(out=P[:, 0:16 * NS], in0=P[:, 0:16 * NS],
                                                           in1=smask[:, 0:16 * NS], op=ALU.mult),
                          reads=[P_b, cb], writes=[P_b])
                    sc.op("dve", lambda e: e.tensor_tensor(out=P[0:NS, 16 * NS:17 * NS], in0=P[0:NS, 16 * NS:17 * NS],
                                                           in1=smask[0:NS, 16 * NS:17 * NS], op=ALU.mult),
                          reads=[P_b, cb], writes=[P_b])
                    for (bk, isz) in ((4, False), (5, True)):
                        for kb in range(16):
                            lhs = ones_b[:, :] if isz else Vc[:, kb * 512 + hh * 128: kb * 512 + (hh + 1) * 128]
                            sc.op("pe", lambda e, bk=bk, kb=kb, lhs=lhs: e.matmul(
                                ps[bk][:, 0:NS], lhs, P[:, kb * NS:(kb + 1) * NS], start=(kb == 0), stop=False),
                                reads=[P_b, c_b, cb], writes=[pb[bk]], signal=False)
                        lhs = ones_b[0:NS, :] if isz else Vs_sb[0:NS, hd * 128:(hd + 1) * 128]
                        sc.op("pe", lambda e, bk=bk, lhs=lhs: e.matmul(ps[bk][:, 0:NS], lhs, P[0:NS, 16 * NS:17 * NS],
                                                                      start=False, stop=True),
                              reads=[P_b, vs_b, cb], writes=[pb[bk]])
                    sc.op("dve", lambda e: e.reciprocal(out=rzs[:], in_=ps[5][:, 0:NS]), reads=[pb[5]], writes=[rz_b])
                    sc.op("dve", lambda e: e.tensor_tensor(out=H.brTs[1][:, hd * NS:(hd + 1) * NS], in0=ps[4][:, 0:NS],
                                                           in1=rzs[:], op=ALU.mult),
                          reads=[pb[4], rz_b], writes=[H.brTs_b[1]])
    sc.barrier()


def _mlstm_run(G, H, es, ntok, Lc, nch, igT, lfT, m0_src, st_src, st_dst, m_dst, loaders, co_store, hng, keep, sel):
    nc, sc, ps, pb = G.nc, G.sc, G.ps, G.pb
    sbt = H.sbt
    ident_f, ident_b, caus_f, cb = G.ident_f, G.ident_b, G.caus_f, G.cb
    gb = Buf()

    def g8(name, n=ntok):
        return sbt(es, name, [8, n], F32)
    m0 = g8("m0", 1)
    if m0_src is None:
        sc.op("dve", lambda e: e.memset(m0[:], 0.0), writes=[gb])
    else:
        sc.dma("sp", m0[:], m0_src, writes=[gb])
    mT, bT, egT, s1T, s2T, emT, uT = (g8("mT"), g8("bT"), g8("egT"), g8("s1T"), g8("s2T"), g8("emT"), g8("uT"))
    mprev, bend, mnew, scdc = g8("mprev", nch), g8("bend", nch), g8("mnew", nch), g8("scdc", 2 * nch)
    V = lambda e: e
    sc.op("dve", lambda e: e.tensor_tensor_scan(out=mT[:], data0=lfT[:, 0:ntok], data1=igT[:, 0:ntok], initial=m0[:, 0:1],
                                                op0=ALU.add, op1=ALU.max), reads=[H.gate_b, gb], writes=[gb])
    sc.op("dve", lambda e: e.tensor_tensor_scan(out=bT[:], data0=keep[:, 0:ntok], data1=lfT[:, 0:ntok], initial=0.0,
                                                op0=ALU.mult, op1=ALU.add), reads=[H.gate_b, gb], writes=[gb])
    sc.op("dve", lambda e: e.tensor_copy(out=mprev[:, 0:1], in_=m0[:, 0:1]), reads=[gb], writes=[gb])
    if nch > 1:
        sc.op("dve", lambda e: e.tensor_copy(
            out=mprev[:, 1:nch], in_=mT[:].rearrange("p (c t) -> p c t", t=Lc)[:, 0:nch - 1, Lc - 1]),
            reads=[gb], writes=[gb])
    sc.op("dve", lambda e: e.tensor_copy(out=bend[:], in_=bT[:].rearrange("p (c t) -> p c t", t=Lc)[:, :, Lc - 1]),
          reads=[gb], writes=[gb])
    sc.op("dve", lambda e: e.tensor_copy(out=mnew[:], in_=mT[:].rearrange("p (c t) -> p c t", t=Lc)[:, :, Lc - 1]),
          reads=[gb], writes=[gb])
    sc.dma("sp", m_dst, mT[:, ntok - 1:ntok], reads=[gb])
    sc.op("dve", lambda e: e.tensor_tensor(out=egT[:], in0=igT[:, 0:ntok], in1=bT[:], op=ALU.subtract),
          reads=[H.gate_b, gb], writes=[gb])
    sc.op("act", lambda e: e.activation(out=egT[:], in_=egT[:], func=AF.Exp), reads=[gb], writes=[gb])
    sc.op("dve", lambda e: e.tensor_tensor(out=uT[:], in0=bT[:], in1=mT[:], op=ALU.subtract), reads=[gb], writes=[gb])
    sc.op("act", lambda e: e.activation(out=s1T[:], in_=uT[:], func=AF.Exp), reads=[gb], writes=[gb])
    sc.op("dve", lambda e: e.tensor_tensor(
        out=s2T[:].rearrange("p (c t) -> p c t", t=Lc), in0=uT[:].rearrange("p (c t) -> p c t", t=Lc),
        in1=mprev[:, :].unsqueeze(2).broadcast_to([8, nch, Lc]), op=ALU.add), reads=[gb], writes=[gb])
    sc.op("act", lambda e: e.activation(out=s2T[:], in_=s2T[:], func=AF.Exp), reads=[gb], writes=[gb])
    sc.op("act", lambda e: e.activation(out=emT[:], in_=mT[:], func=AF.Exp, scale=-1.0), reads=[gb], writes=[gb])
    sc.op("dve", lambda e: e.tensor_tensor(out=scdc[:, 0:nch], in0=bend[:], in1=mnew[:], op=ALU.subtract),
          reads=[gb], writes=[gb])
    sc.op("dve", lambda e: e.tensor_tensor(out=scdc[:, nch:2 * nch], in0=scdc[:, 0:nch], in1=mprev[:], op=ALU.add),
          reads=[gb], writes=[gb])
    sc.op("act", lambda e: e.activation(out=scdc[:], in_=scdc[:], func=AF.Exp), reads=[gb], writes=[gb])
    cols = sbt(es, "mcols", [128, nch * 32], F32)
    bcs = sbt(es, "mbcs", [128, 8 * 2 * nch], F32)
    for c in range(nch):
        for qi, Q in enumerate((egT, s1T, s2T, emT)):
            sc.op("pe", lambda e, c=c, qi=qi, Q=Q: e.matmul(
                ps[5][0:Lc, (c * 4 + qi) * 8:(c * 4 + qi + 1) * 8], Q[:, c * Lc:(c + 1) * Lc], ident_f[0:8, 0:8],
                start=True, stop=True), reads=[gb, cb], writes=[pb[5]])
    sc.op("act", lambda e: e.activation(out=cols[0:Lc, :], in_=ps[5][0:Lc, 0:nch * 32], func=AF.Copy),
          reads=[pb[5]], writes=[gb])
    for hd in range(8):
        sc.op("pe", lambda e, hd=hd: e.matmul(ps[6][:, hd * 2 * nch:(hd + 1) * 2 * nch], sel[:, hd * 128:(hd + 1) * 128],
                                              scdc[:, :], start=True, stop=True), reads=[gb], writes=[pb[6]])
    sc.op("act", lambda e: e.activation(out=bcs[:], in_=ps[6][:, 0:16 * nch], func=AF.Copy), reads=[pb[6]], writes=[gb])

    Cst = sbt(es, "mCst", [128, 257], F32)
    Cbf = sbt(es, "mCbf", [128, 257], BF16)
    C_b = Buf()
    PT = sbt(es, "mPT", [128, 128], BF16)
    ktl = sbt(es, "mktl", [128, 128], BF16)
    hn1 = sbt(es, "mhn1", [128, 257], F32)
    hn2 = sbt(es, "mhn2", [128, 257], F32)
    dn = sbt(es, "mdn", [128, 4], F32)
    junk = sbt(es, "mjunk", [128, 256], F32)
    hnn = sbt(es, "mhnn", [128, 256], BF16)
    w_b = Buf()
    for hd in range(8):
        qT_t, kT_t, kc_t, vx_t, og_t, in_b = loaders(hd)
        if st_src is None:
            sc.op("dve", lambda e: e.memset(Cst[:], 0.0), writes=[C_b])
        else:
            sc.dma("sp", Cst[:], st_src[hd], writes=[C_b])
        sc.op("act", lambda e: e.activation(out=Cbf[:], in_=Cst[:], func=AF.Copy), reads=[C_b], writes=[C_b])
        for c in range(nch):
            tsl = slice(c * Lc, (c + 1) * Lc)
            col = lambda qi: cols[0:Lc, (c * 4 + qi) * 8 + hd:(c * 4 + qi) * 8 + hd + 1]
            sc.op("pe", lambda e: e.matmul(ps[0][0:Lc, 0:Lc], kT_t[:, tsl], qT_t[:, tsl], start=True, stop=True),
                  reads=[in_b], writes=[pb[0]])
            sc.op("dve", lambda e: e.scalar_tensor_tensor(out=PT[0:Lc, 0:Lc], in0=ps[0][0:Lc, 0:Lc], scalar=col(0),
                                                          in1=caus_f[0:Lc, 0:Lc], op0=ALU.mult, op1=ALU.mult),
                  reads=[pb[0], gb, cb], writes=[w_b])
            sc.op("act", lambda e: e.activation(out=ktl[0:Lc, :], in_=kc_t[0:Lc, c * 128:(c + 1) * 128], func=AF.Copy,
                                                scale=col(0)), reads=[in_b, gb], writes=[w_b])
            sc.op("pe", lambda e: e.matmul(ps[1][0:Lc, 0:257], PT[0:Lc, 0:Lc], vx_t[0:Lc, c * 257:(c + 1) * 257],
                                           start=True, stop=True), reads=[w_b, in_b], writes=[pb[1]])
            sc.op("pe", lambda e: e.matmul(ps[2][0:Lc, 0:257], qT_t[:, tsl], Cbf[:, :], start=True, stop=True),
                  reads=[in_b, C_b], writes=[pb[2]])
            sc.op("dve", lambda e: e.tensor_scalar(out=hn1[0:Lc, :], in0=ps[1][0:Lc, 0:257], scalar1=col(1), scalar2=None,
                                                   op0=ALU.mult), reads=[pb[1], gb], writes=[w_b])
            sc.op("dve", lambda e: e.scalar_tensor_tensor(out=hn2[0:Lc, :], in0=ps[2][0:Lc, 0:257], scalar=col(2),
                                                          in1=hn1[0:Lc, :], op0=ALU.mult, op1=ALU.add),
                  reads=[pb[2], gb, w_b], writes=[w_b])
            sc.op("act", lambda e: e.activation(out=dn[0:Lc, 0:1], in_=hn2[0:Lc, 256:257], func=AF.Abs),
                  reads=[w_b, gb], writes=[w_b])
            sc.op("dve", lambda e: e.tensor_tensor(out=dn[0:Lc, 0:1], in0=dn[0:Lc, 0:1], in1=col(3), op=ALU.max),
                  reads=[w_b, gb], writes=[w_b])
            sc.op("dve", lambda e: e.reciprocal(out=dn[0:Lc, 0:1], in_=dn[0:Lc, 0:1]), reads=[w_b], writes=[w_b])
            sc.op("act", lambda e: e.activation(out=junk[0:Lc, :], in_=hn2[0:Lc, 0:256], func=AF.Square,
                                                scale=dn[0:Lc, 0:1], accum_out=dn[0:Lc, 1:2]),
                  reads=[w_b], writes=[w_b])
            sc.op("act", lambda e: e.activation(out=dn[0:Lc, 1:2], in_=dn[0:Lc, 1:2], func=AF.Sqrt, bias=EPS,
                                                scale=1.0 / 256), reads=[w_b], writes=[w_b])
            sc.op("dve", lambda e: e.reciprocal(out=dn[0:Lc, 1:2], in_=dn[0:Lc, 1:2]), reads=[w_b], writes=[w_b])
            sc.op("dve", lambda e: e.tensor_tensor(out=dn[0:Lc, 2:3], in0=dn[0:Lc, 0:1], in1=dn[0:Lc, 1:2], op=ALU.mult),
                  reads=[w_b], writes=[w_b])
            sc.op("dve", lambda e: e.scalar_tensor_tensor(out=hnn[0:Lc, :], in0=hn2[0:Lc, 0:256], scalar=dn[0:Lc, 2:3],
                                                          in1=hng[0:Lc, hd * 256:(hd + 1) * 256], op0=ALU.mult,
                                                          op1=ALU.mult), reads=[w_b, gb], writes=[w_b])
            for a in range(2):
                sc.op("pe", lambda e, a=a: e.matmul(ps[3][:, a * Lc:(a + 1) * Lc], hnn[0:Lc, a * 128:(a + 1) * 128],
                                                    ident_b[0:Lc, 0:Lc], start=True, stop=True),
                      reads=[w_b, cb], writes=[pb[3]])
            co_store(hd, c, ps[3], pb[3], og_t, in_b)
            sc.op("pe", lambda e: e.matmul(ps[4][:, 0:257], ktl[0:Lc, :], vx_t[0:Lc, c * 257:(c + 1) * 257],
                                           start=True, stop=True), reads=[w_b, in_b], writes=[pb[4]])
            dcc = bcs[:, hd * 2 * nch + nch + c: hd * 2 * nch + nch + c + 1]
            scc = bcs[:, hd * 2 * nch + c: hd * 2 * nch + c + 1]
            sc.op("dve", lambda e: e.tensor_scalar(out=Cst[:], in0=Cst[:], scalar1=dcc, scalar2=None, op0=ALU.mult),
                  reads=[gb, C_b], writes=[C_b])
            sc.op("dve", lambda e: e.scalar_tensor_tensor(out=Cst[:], in0=ps[4][:, 0:257], scalar=scc, in1=Cst[:],
                                                          op0=ALU.mult, op1=ALU.add), reads=[pb[4], gb, C_b], writes=[C_b])
            sc.op("act", lambda e: e.activation(out=Cbf[:], in_=Cst[:], func=AF.Copy), reads=[C_b], writes=[C_b])
        sc.dma("sp", st_dst[hd], Cst[:], reads=[C_b])


def stage_mlstm(G, H):
    nc, sc, ws, ps, pb = G.nc, G.sc, G.ws, G.ps, G.pb
    l, h, tok0, do_s, sbt = G.l, H.h, H.tok0, H.do_s, H.sbt
    with contextlib.ExitStack() as es:
        hng = sbt(es, "hng", [128, 2048], F32)
        keep = sbt(es, "keep", [8, 1024], F32)
        sel = sbt(es, "sel", [8, 1024], F32)
        k_b = Buf()
        sc.dma("sp", hng[:], G.hng_in[l], writes=[H.gate_b])
        sc.dma("sp", keep[:], G.c_keep[:, :], writes=[H.gate_b])
        sc.dma("sp", sel[:], G.c_sel[:, :], writes=[H.gate_b])
        with contextlib.ExitStack() as e2:
            qT_t = [sbt(e2, "mq%d" % i, [128, TH], BF16) for i in range(2)]
            kT_t = [sbt(e2, "mk%d" % i, [128, TH], BF16) for i in range(2)]
            kc_t = [sbt(e2, "mkc%d" % i, [128, TH], BF16) for i in range(2)]
            vx_t = [sbt(e2, "mvx%d" % i, [128, NSUB * 257], BF16) for i in range(2)]
            og_t = [sbt(e2, "mog%d" % i, [128, 2 * TH], BF16) for i in range(2)]
            co = [sbt(e2, "mco%d" % i, [128, 2 * TH], BF16) for i in range(2)]
            in_b = bufs(2)
            co_b = bufs(2)
            for i in range(2):
                sc.op("dve", lambda e, i=i: e.memset(vx_t[i][:], 1.0), writes=[in_b[i]])

            def loaders(hd):
                i = hd % 2
                tk = slice(tok0, tok0 + TH)
                sc.dma("sp", qT_t[i][:], G.qcT_d[hd * 128:(hd + 1) * 128, tk], writes=[in_b[i]])
                sc.dma("sp", kT_t[i][:], G.kcT_d[hd * 128:(hd + 1) * 128, tk], writes=[in_b[i]])
                sc.dma("sp", kc_t[i][:].rearrange("p (c k) -> p c k", c=NSUB),
                       G.kc_d[tk, hd * 128:(hd + 1) * 128].rearrange("(c p) k -> p c k", p=128), writes=[in_b[i]])
                sc.dma("sp", vx_t[i][:].rearrange("p (c v) -> p c v", c=NSUB)[:, :, 0:256],
                       G.vc_d[tk, hd * 256:(hd + 1) * 256].rearrange("(c p) v -> p c v", p=128), writes=[in_b[i]])
                sc.dma("sp", og_t[i][:].rearrange("p (a t) -> p a t", a=2),
                       G.ogT_d[hd * 256:(hd + 1) * 256, tk].rearrange("(a p) t -> p a t", p=128), writes=[in_b[i]])
                return qT_t[i], kT_t[i], kc_t[i], vx_t[i], og_t[i], in_b[i]

            def co_store(hd, c, pst, pstb, og, ib):
                i = hd % 2
                sc.op("dve", lambda e: e.tensor_tensor(
                    out=co[i][:].rearrange("p (a t) -> p a t", a=2)[:, :, c * 128:(c + 1) * 128],
                    in0=pst[:, 0:256].rearrange("p (a t) -> p a t", a=2),
                    in1=og[:].rearrange("p (a t) -> p a t", a=2)[:, :, c * 128:(c + 1) * 128], op=ALU.mult),
                    reads=[pstb, ib], writes=[co_b[i]])
                if c == NSUB - 1:
                    sc.dma("sp", G.brT_d[2][hd * 256:(hd + 1) * 256, tok0:tok0 + TH].rearrange("(a p) t -> p a t", p=128),
                           co[i][:].rearrange("p (a t) -> p a t", a=2), reads=[co_b[i]])
            st_src = None if h == 0 else [G.st_out[l, hd] for hd in range(8)]
            m_src = None if h == 0 else G.m_out[l]
            _mlstm_run(G, H, e2, TH, 128, NSUB, H.igT, H.lfT, m_src, st_src,
                       [G.st_out[l, hd] for hd in range(8)], G.m_out[l], loaders, co_store, hng, keep, sel)
        sc.barrier()
        if do_s:
            with contextlib.ExitStack() as e3:
                kcs = sbt(e3, "skc", [NS, 1024], BF16)
                vxs = sbt(e3, "svx", [NS, 8 * 257], BF16)
                s_b = Buf()
                sc.op("dve", lambda e: e.memset(vxs[:], 1.0), writes=[s_b])
                sc.dma("sp", kcs[:], G.kcs_d[:, :], writes=[s_b])
                sc.dma("sp", vxs[:].rearrange("p (c v) -> p c v", c=8)[:, :, 0:256],
                       G.vcs_d[:, :].rearrange("p (c v) -> p c v", c=8), writes=[s_b])

                def loaders_s(hd):
                    return (H.qcTs[:, hd * NS:(hd + 1) * NS], H.kcTs[:, hd * NS:(hd + 1) * NS],
                            kcs[:, hd * 128:(hd + 1) * 128], vxs[:, hd * 257:(hd + 1) * 257],
                            H.ogTs[:, hd * 2 * NS:(hd + 1) * 2 * NS], s_b)

                def co_store_s(hd, c, pst, pstb, og, ib):
                    sc.op("dve", lambda e: e.tensor_tensor(out=H.brTs[2][:, hd * 2 * NS:(hd + 1) * 2 * NS],
                                                           in0=pst[:, 0:2 * NS], in1=og, op=ALU.mult),
                          reads=[pstb, H.smp_b], writes=[H.brTs_b[2]])
                _mlstm_run(G, H, e3, NS, NS, 1, H.igTs, H.lfTs, G.m_in[l], [G.st_in[l, hd] for hd in range(8)],
                           [G.sts_out[l, hd] for hd in range(8)], G.ms_out[l], loaders_s, co_store_s, hng, keep, sel)
    sc.barrier()


def _rmw_epi(G, H, es, xt, xt_b, cnt):
    sc, ps, pb = G.sc, G.ps, G.pb
    tok0, h, do_s = H.tok0, H.h, H.do_s

    def epi(sec, f, m, bA, bB, bS):
        fb = f // 128
        i = cnt["x"] % len(xt)
        cnt["x"] += 1
        yb = G.yT_b[fb][h]
        sc.dma("sp", xt[i][:], G.yT[f:f + 128, tok0:tok0 + TH], reads=[yb], writes=[xt_b[i]])
        for bk, blk in ((bA, 0), (bB, 1)):
            sl = slice(blk * 512, (blk + 1) * 512)
            sc.op("dve", lambda e, bk=bk, sl=sl: e.tensor_tensor(out=xt[i][:, sl], in0=ps[bk][:, :], in1=xt[i][:, sl],
                                                               op=ALU.add), reads=[pb[bk], xt_b[i]], writes=[xt_b[i]])
        sc.dma("sp", G.yT[f:f + 128, tok0:tok0 + TH], xt[i][:], reads=[xt_b[i]], writes=[yb])
        if do_s:
            sc.op("dve", lambda e: e.tensor_tensor(out=G.xsT[:, fb * NS:(fb + 1) * NS], in0=ps[bS][:, 0:NS],
                                                   in1=G.xsT[:, fb * NS:(fb + 1) * NS], op=ALU.add),
                  reads=[pb[bS], G.xsT_b], writes=[G.xsT_b])
    return epi


def stage_merge(G, H):
    nc, sc, ws, ps, pb = G.nc, G.sc, G.ws, G.ps, G.pb
    l, h, tok0, do_s, sbt = G.l, H.h, H.tok0, H.do_s, H.sbt
    with contextlib.ExitStack() as es:
        mT = sbt(es, "mT", [128, 32 * 1024], BF16)
        mT_b = bufs(2)
        mTs = sbt(es, "mTs", [128, 32 * NS], BF16)
        mTs_b = Buf()
        brT = sbt(es, "brT", [128, 16 * 1024], BF16)
        brT_b = bufs(2)
        gst = [sbt(es, "gst%d" % i, [128, 1024], BF16) for i in range(2)]
        gst_b = bufs(2)
        tmpm = sbt(es, "tmpm", [128, 512], BF16)
        tmpm_b = Buf()
        tmps = sbt(es, "tmpms", [128, NS], BF16)
        cnt = dict(g=0, x=0)
        for br in range(3):
            sc.dma("sp", H.kview(brT[:], 16),
                   G.brT_d[br][:, tok0:tok0 + TH].rearrange("(k p) t -> p k t", p=128),
                   writes=[brT_b[0], brT_b[1]])

            def epi(sec, f, m, bA, bB, bS, br=br):
                fb = f // 128
                i = cnt["g"] % 2
                cnt["g"] += 1
                sc.dma("sp", gst[i][:], G.gT_d[br * D + f: br * D + f + 128, tok0:tok0 + TH], writes=[gst_b[i]])
                for bk, blk in ((bA, 0), (bB, 1)):
                    sl = slice(blk * 512, (blk + 1) * 512)
                    dst = mT[:, fb * 1024 + blk * 512: fb * 1024 + (blk + 1) * 512]
                    if br == 0:
                        sc.op("dve", lambda e, bk=bk, sl=sl, dst=dst: e.tensor_tensor(
                            out=dst, in0=ps[bk][:, :], in1=gst[i][:, sl], op=ALU.mult),
                            reads=[pb[bk], gst_b[i]], writes=[mT_b[blk]])
                    else:
                        sc.op("dve", lambda e, bk=bk, sl=sl: e.tensor_tensor(
                            out=tmpm[:], in0=ps[bk][:, :], in1=gst[i][:, sl], op=ALU.mult),
                            reads=[pb[bk], gst_b[i]], writes=[tmpm_b])
                        sc.op("dve", lambda e, dst=dst: e.tensor_tensor(out=dst, in0=dst, in1=tmpm[:], op=ALU.add),
                              reads=[tmpm_b, mT_b[blk]], writes=[mT_b[blk]])
                if do_s:
                    gsl = H.gTs[:, (br * 32 + fb) * NS:(br * 32 + fb + 1) * NS]
                    dsts = mTs[:, fb * NS:(fb + 1) * NS]
                    if br == 0:
                        sc.op("dve", lambda e: e.tensor_tensor(out=dsts, in0=ps[bS][:, 0:NS], in1=gsl, op=ALU.mult),
                              reads=[pb[bS], H.smp_b], writes=[mTs_b])
                    else:
                        sc.op("dve", lambda e: e.tensor_tensor(out=tmps[:], in0=ps[bS][:, 0:NS], in1=gsl, op=ALU.mult),
                              reads=[pb[bS], H.smp_b], writes=[tmpm_b])
                        sc.op("dve", lambda e: e.tensor_tensor(out=dsts, in0=dsts, in1=tmps[:], op=ALU.add),
                              reads=[tmpm_b, mTs_b], writes=[mTs_b])

            wl = G.w_br[br][l].rearrange("(k p) n -> p k n", p=128)
            tiles = [("br", c, 512, "fm", c) for c in range(0, D, 512)]
            H.gemm(es, tiles, lambda c0, n, wl=wl: wl[:, :, c0:c0 + n], 16, brT, brT_b, H.brTs[br], H.brTs_b[br],
                   epi, None)
        xt = [sbt(es, "xt%d" % i, [128, 1024], F32) for i in range(2)]
        xt_b = bufs(2)
        wl = G.w_out[l].rearrange("(k p) n -> p k n", p=128)
        tiles = [("wo", c, 256, "fm", c) for c in range(0, D, 256)]
        H.gemm(es, tiles, lambda c0, n: wl[:, :, c0:c0 + n], 32, mT, mT_b, mTs, mTs_b,
               _rmw_epi(G, H, es, xt, xt_b, cnt), None)
    sc.barrier()


def stage_ffn(G, H):
    nc, sc, ws, ps, pb = G.nc, G.sc, G.ws, G.ps, G.pb
    l, h, tok0, do_s, sbt = G.l, H.h, H.tok0, H.do_s, H.sbt
    with contextlib.ExitStack() as es:
        xnT = sbt(es, "xn2T", [128, 32 * 1024], BF16)
        xnT_b = bufs(2)
        H.rmsnorm(es, H.g2, xnT, xnT_b, G.xnTs)
        hT = sbt(es, "hT", [128, 16 * 1024], BF16)
        hT_b = bufs(2)
        hTs = sbt(es, "hTs", [128, 16 * NS], BF16)
        hTs_b = Buf()
        rl = [sbt(es, "rl%d" % i, [128, 512], F32) for i in range(2)]
        rl_b = bufs(2)
        rls = sbt(es, "rls", [128, NS], F32)
        rls_b = Buf()
        xt = [sbt(es, "xt%d" % i, [128, 1024], F32) for i in range(2)]
        xt_b = bufs(2)
        cnt = dict(r=0, x=0)
        w1 = G.w_ff1[l].rearrange("(k p) n -> p k n", p=128)
        for kg in range(8):
            def epi1(sec, f, m, bA, bB, bS):
                fb = f // 128
                for bk, blk in ((bA, 0), (bB, 1)):
                    i = cnt["r"] % 2
                    cnt["r"] += 1
                    dst = hT[:, fb * 1024 + blk * 512: fb * 1024 + (blk + 1) * 512]
                    sc.op("act", lambda e, bk=bk, i=i: e.activation(out=rl[i][:], in_=ps[bk][:, :], func=AF.Relu),
                          reads=[pb[bk]], writes=[rl_b[i]])
                    sc.op("dve", lambda e, i=i, dst=dst: e.tensor_tensor(out=dst, in0=rl[i][:], in1=rl[i][:], op=ALU.mult),
                          reads=[rl_b[i]], writes=[hT_b[blk]])
                if do_s:
                    sc.op("act", lambda e: e.activation(out=rls[:], in_=ps[bS][:, 0:NS], func=AF.Relu),
                          reads=[pb[bS]], writes=[rls_b])
                    sc.op("dve", lambda e: e.tensor_tensor(out=hTs[:, fb * NS:(fb + 1) * NS], in0=rls[:], in1=rls[:],
                                                           op=ALU.mult), reads=[rls_b], writes=[hTs_b])
            tiles = [("f1", kg * 2048 + c, 256, "fm", c) for c in range(0, 2048, 256)]
            H.gemm(es, tiles, lambda c0, n: w1[:, :, c0:c0 + n], 32, xnT, xnT_b, G.xnTs, G.xnTs_b, epi1, None)
            w2 = G.w_ff2[l][kg * 2048:(kg + 1) * 2048, :].rearrange("(k p) n -> p k n", p=128)
            tiles = [("f2", c, 512, "fm", c) for c in range(0, D, 512)]
            H.gemm(es, tiles, lambda c0, n, w2=w2: w2[:, :, c0:c0 + n], 16, hT, hT_b, hTs, hTs_b,
                   _rmw_epi(G, H, es, xt, xt_b, cnt), None)
    sc.barrier()


_NC = None
_DBG = None


def kernel(x_prompt, x_sample, cache_swa_k, cache_swa_v, state_mlstm_C, state_mlstm_n, state_mlstm_m,
           norm1_g, w_in, ws_a, bs_a, norm_va_g, qn_g, kn_g, i_b, f_b, hn_c_g,
           w_branch_a, w_branch_b, w_branch_c, w_out, norm2_g, w_ff1, w_ff2):
    global _NC
    f32 = np.float32
    A = lambda a: np.ascontiguousarray(np.asarray(a, dtype=f32))
    if _NC is None:
        _NC = build()
    nc = _NC
    L = DEPTH
    consts = host_consts()
    shared = dict(
        w_in=A(w_in), w_b0=A(w_branch_a), w_b1=A(w_branch_b), w_b2=A(w_branch_c), w_out=A(w_out),
        w_ff1=A(w_ff1), w_ff2=A(w_ff2),
        g1=A(np.asarray(norm1_g).reshape(L, 32, 128).transpose(0, 2, 1)),
        g2=A(np.asarray(norm2_g).reshape(L, 32, 128).transpose(0, 2, 1)),
        gva=A(np.broadcast_to(np.asarray(norm_va_g)[:, None, :], (L, 128, 2048))),
        hng=A(np.broadcast_to(np.asarray(hn_c_g)[:, None, :], (L, 128, 2048))),
        gq=A(np.asarray(qn_g).reshape(L, 128, 1)), gk=A(np.asarray(kn_g).reshape(L, 128, 1)),
        ib=A(np.asarray(i_b).reshape(L, 8, 1)), fb=A(np.asarray(f_b).reshape(L, 8, 1)),
        wsT=A(np.asarray(ws_a).transpose(0, 3, 1, 2).reshape(L, 128, 1024)),
        bT=A(np.broadcast_to(np.asarray(bs_a)[:, None, :, :], (L, 128, 8, 128)).reshape(L, 128, 1024)),
    )
    shared.update(consts)
    xp = np.asarray(x_prompt)
    xs = np.asarray(x_sample)
    ck = np.asarray(cache_swa_k)
    cv = np.asarray(cache_swa_v)
    sC = np.asarray(state_mlstm_C)
    sn = np.asarray(state_mlstm_n)
    sm = np.asarray(state_mlstm_m)
    if KSTOP <= 10:
        shared["w_in"] = A(np.asarray(w_in)[0:1])
        for k_, a_ in (("w_b0", w_branch_a), ("w_b1", w_branch_b), ("w_b2", w_branch_c), ("w_out", w_out)):
            shared[k_] = A(np.asarray(a_)[0:1]) if KSTOP >= 6 else np.zeros((1, 1, 1), f32)
        for k_, a_ in (("w_ff1", w_ff1), ("w_ff2", w_ff2)):
            shared[k_] = A(np.asarray(a_)[0:1]) if KSTOP >= 7 else np.zeros((1, 1, 1), f32)
    in_maps = []
    for c in range(8):
        b = c % 4
        d = dict(shared)
        d["xT"] = A(xp[b].T)
        d["xsT"] = A(xs[c].T)
        d["ck"] = A(ck[:, c].reshape(L, 2048, 2048))
        d["cv"] = A(cv[:, c].reshape(L, 2048, 2048))
        d["st_in"] = A(np.concatenate([sC[:, c].transpose(0, 1, 3, 2), sn[:, c][..., None]], axis=-1))
        d["m_in"] = A(sm[:, c].reshape(L, 8, 1))
        in_maps.append(d)
    res = run_bass_kernel_spmd(nc, in_maps, core_ids=list(range(8)))
    R = res.results
    global _DBG
    _DBG = [r.get("dbg") for r in R]
    y_p = np.stack([R[b]["yT"].T for b in range(4)]).astype(f32)
    y_s = np.stack([R[c]["ysT"].T for c in range(8)]).astype(f32)
    k_p = np.stack([np.stack([R[b]["kT_out"][l].T.reshape(S, 16, 128) for b in range(4)]) for l in range(L)])
    v_p = np.stack([np.stack([R[b]["v_out"][l].reshape(S, 16, 128) for b in range(4)]) for l in range(L)])
    C_p = np.stack([np.stack([R[b]["st_out"][l][:, :, :256].transpose(0, 2, 1) for b in range(4)]) for l in range(L)])
    n_p = np.stack([np.stack([R[b]["st_out"][l][:, :, 256] for b in range(4)]) for l in range(L)])
    m_p = np.stack([np.stack([R[b]["m_out"][l][:, 0] for b in range(4)]) for l in range(L)])
    k_s = np.stack([np.stack([R[c]["ks_out"][l].T.reshape(NS, 16, 128) for c in range(8)]) for l in range(L)])
    v_s = np.stack([np.stack([R[c]["vs_out"][l].reshape(NS, 16, 128) for c in range(8)]) for l in range(L)])
    C_s = np.stack([np.stack([R[c]["sts_out"][l][:, :, :256].transpose(0, 2, 1) for c in range(8)]) for l in range(L)])
    n_s = np.stack([np.stack([R[c]["sts_out"][l][:, :, 256] for c in range(8)]) for l in range(L)])
    m_s = np.stack([np.stack([R[c]["ms_out"][l][:, 0] for c in range(8)]) for l in range(L)])
    va_s = np.stack([np.stack([R[c]["va_out"][l] for c in range(8)]) for l in range(L)])
    outs = (y_p, y_s, k_p, v_p, C_p, n_p, m_p, k_s, v_s, C_s, n_s, m_s, va_s)
    return tuple(np.ascontiguousarray(o, dtype=f32) for o in outs)
```

```python
import contextlib
import numpy as np
import ml_dtypes
import concourse.bass as bass
import concourse.mybir as mybir
from concourse.bass_utils import run_bass_kernel_spmd

F32 = mybir.dt.float32
BF16 = mybir.dt.bfloat16
AF = mybir.ActivationFunctionType
ALU = mybir.AluOpType
AX = mybir.AxisListType

D = 4096
S = 2048
TH = 1024
NSUB = TH // 128
DEPTH = 2
NS = 4
DIN = 28688
DFF = 16384
EPS = 1e-6
O_UA, O_VA, O_QB, O_KB, O_VB, O_QC, O_KC, O_VC, O_OC, O_IC, O_FC, O_G = (
    0, 2048, 4096, 6144, 8192, 10240, 11264, 12288, 14336, 16384, 16392, 16400)
SAME_ENGINE_SYNC = True
import os
KSTOP = int(os.environ.get('KSTOP', '99'))
KTILES = int(os.environ.get('KTILES', '9999'))


class Buf:
    __slots__ = ("w", "r")

    def __init__(self):
        self.w = {}
        self.r = {}


def bufs(n):
    return [Buf() for _ in range(n)]


class Sched:
    def __init__(self, nc, n_sp=28, n_pool=12):
        self.nc = nc
        self.eng = {"pe": nc.tensor, "act": nc.scalar, "dve": nc.vector,
                    "pool": nc.gpsimd, "sp": nc.sync}
        self.sem = {}
        for e in ("pe", "act", "dve"):
            self.sem[e] = nc.alloc_semaphore("s_" + e)
        self.cnt = {e: 0 for e in ("pe", "act", "dve")}
        self.waited = {e: {} for e in self.eng}
        self.lanes = {"sp": [], "pool": []}
        for q, n in (("sp", n_sp), ("pool", n_pool)):
            for i in range(n):
                k = "%s%d" % (q, i)
                self.sem[k] = nc.alloc_semaphore("l_" + k)
                self.lanes[q].append(k)
        self.lane_cnt = {k: 0 for q in self.lanes for k in self.lanes[q]}
        self.lane_next = {"sp": 0, "pool": 0}
        self.pending = {e: set() for e in self.eng}
        self.n_instr = 0

    def _wait(self, e, deps):
        eng = self.eng[e]
        deps = set(deps) | self.pending[e]
        self.pending[e] = set()
        w = self.waited[e]
        for (k, v) in sorted(deps):
            if k == e and (e == "pe" or not SAME_ENGINE_SYNC):
                continue
            if w.get(k, 0) >= v:
                continue
            eng.wait_ge(self.sem[k], v)
            w[k] = v

    @staticmethod
    def _deps(reads, writes):
        deps = set()
        for b in reads:
            deps.update(b.w.values())
        for b in writes:
            deps.update(b.w.values())
            deps.update(b.r.values())
        return deps

    def op(self, e, fn, reads=(), writes=(), signal=True):
        self._wait(e, self._deps(reads, writes))
        ins = fn(self.eng[e])
        self.n_instr += 1
        if signal:
            self.cnt[e] += 1
            ins.then_inc(self.sem[e], 1)
            tok = (e, self.cnt[e])
        else:
            tok = (e, self.cnt[e] + 1)
        for b in reads:
            b.r[e] = tok
        for b in writes:
            b.w = {e: tok}
            b.r = {}
        return tok

    def dma(self, q, out, in_, reads=(), writes=(), accum=False, after_barrier=False):
        lanes = self.lanes[q]
        k = lanes[self.lane_next[q] % len(lanes)]
        self.lane_next[q] += 1
        deps = self._deps(reads, writes)
        if after_barrier:
            deps |= getattr(self, "last_barrier", set())
        if self.lane_cnt[k]:
            deps.add((k, self.lane_cnt[k] * 16))
        self._wait(q, deps)
        kw = {}
        if accum:
            kw["accum_op"] = ALU.add
        ins = self.eng[q].dma_start(out=out, in_=in_, **kw)
        self.n_instr += 1
        self.lane_cnt[k] += 1
        ins.then_inc(self.sem[k], 16)
        tok = (k, self.lane_cnt[k] * 16)
        for b in reads:
            b.r[k] = tok
        for b in writes:
            b.w = {k: tok}
            b.r = {}
        return tok

    def all_tokens(self, with_pool=False):
        toks = set()
        for e in ("pe", "act", "dve"):
            if self.cnt[e]:
                toks.add((e, self.cnt[e]))
        qs = ("sp", "pool") if with_pool else ("sp",)
        for q in qs:
            for k in self.lanes[q]:
                if self.lane_cnt[k]:
                    toks.add((k, self.lane_cnt[k] * 16))
        return toks

    def barrier(self):
        toks = self.all_tokens()
        self.last_barrier = set(toks)
        for e in ("pe", "act", "dve", "sp"):
            self.pending[e] |= toks

    def finish(self):
        toks = self.all_tokens(with_pool=True)
        self._wait("sp", toks)
        self.eng["sp"].wait_ge(self.sem[self.lanes["sp"][0]], 0)


class WStream:
    def __init__(self, sc, nc, nbuf=3, depth=2):
        self.sc = sc
        self.nbuf = nbuf
        self.depth = depth
        self.t = [nc.alloc_sbuf_tensor("wb%d" % i, [128, 8192], BF16) for i in range(nbuf)]
        self.b = bufs(nbuf)
        self.plan = []
        self.issued = 0
        self.used = 0

    def _issue(self):
        src, kc, ncol, rows = self.plan[self.issued]
        i = self.issued % self.nbuf
        dst = self.t[i][0:rows, 0:kc * ncol].rearrange("p (k n) -> p k n", k=kc)
        self.sc.dma("pool", dst, src, writes=[self.b[i]])
        self.issued += 1

    def next(self, key):
        assert self.plan[self.used][-1] == key or True
        while self.issued < len(self.plan) and self.issued <= self.used + self.depth:
            self._issue()
        src, kc, ncol, rows = self.plan[self.used]
        i = self.used % self.nbuf
        self.used += 1
        view = self.t[i][0:rows, 0:kc * ncol].rearrange("p (k n) -> p k n", k=kc)
        return view, self.b[i]


def c_mult(d):
    d = np.asarray(d)
    c = ((d >= 0) & (d <= 128)).astype(np.float32)
    c += ((d >= 0) & (d <= 512) & (d % 4 == 0)).astype(np.float32)
    c += ((d >= 0) & (d <= 2048) & (d % 16 == 0)).astype(np.float32)
    return c


def host_consts():
    k = np.arange(128)[:, None, None]
    dl = np.arange(16)[None, :, None]
    q = np.arange(128)[None, None, :]
    amask = c_mult(dl * 128 + q - k).reshape(128, 16 * 128).astype(np.float32)
    kk = np.arange(128)[:, None, None]
    blk = np.arange(17)[None, :, None]
    t = np.arange(NS)[None, None, :]
    kg = blk * 128 + kk
    sm = c_mult(2048 + t - kg)
    sm = np.where((blk == 16) & (kk >= NS), 0.0, sm)
    smask = sm.reshape(128, 17 * NS).astype(np.float32)
    s_ = np.arange(128)[:, None]
    t_ = np.arange(128)[None, :]
    caus = (s_ <= t_).astype(np.float32)
    keep = np.ones((8, 1024), np.float32)
    keep[:, ::128] = 0.0
    sel = np.zeros((8, 8, 128), np.float32)
    for h in range(8):
        sel[h, h, :] = 1.0
    return dict(c_ident=np.eye(128, dtype=np.float32), c_amask=amask, c_smask=smask,
                c_caus=caus, c_keep=keep, c_sel=sel.reshape(8, 1024))


def build():
    nc = bass.Bass("TRN2", target_bir_lowering=False)
    sc = Sched(nc)
    L = DEPTH

    def din(name, shape, dt=F32):
        return nc.dram_tensor(name, list(shape), dt, kind="ExternalInput").ap()

    def dout(name, shape, dt=F32):
        return nc.dram_tensor(name, list(shape), dt, kind="ExternalOutput").ap()

    def dscr(name, shape, dt=BF16):
        return nc.dram_tensor(name, list(shape), dt, kind="Internal").ap()

    def sb(name, shape, dt):
        return nc.alloc_sbuf_tensor(name, list(shape), dt)

    xT_in = din("xT", [D, S])
    xsT_in = din("xsT", [D, NS])
    SMALL = KSTOP <= 5
    L1W = 1 if KSTOP <= 10 else L
    ck_in = din("ck", [L, 2048, 2048])
    cv_in = din("cv", [L, 2048, 2048])
    st_in = din("st_in", [L, 8, 128, 257])
    m_in = din("m_in", [L, 8, 1])
    w_in = din("w_in", [L1W, D, DIN])
    w_br = [din("w_b%d" % i, [L1W, 2048, D] if KSTOP >= 6 else [1, 1, 1]) for i in range(3)]
    w_out = din("w_out", [L1W, D, D] if KSTOP >= 6 else [1, 1, 1])
    w_ff1 = din("w_ff1", [L1W, D, DFF] if KSTOP >= 7 else [1, 1, 1])
    w_ff2 = din("w_ff2", [L1W, DFF, D] if KSTOP >= 7 else [1, 1, 1])
    g1_in = din("g1", [L, 128, 32])
    g2_in = din("g2", [L, 128, 32])
    gva_in = din("gva", [L, 128, 2048])
    hng_in = din("hng", [L, 128, 2048])
    gq_in = din("gq", [L, 128, 1])
    gk_in = din("gk", [L, 128, 1])
    ib_in = din("ib", [L, 8, 1])
    fb_in = din("fb", [L, 8, 1])
    wsT_in = din("wsT", [L, 128, 1024])
    bT_in = din("bT", [L, 128, 1024])
    c_ident = din("c_ident", [128, 128])
    c_amask = din("c_amask", [128, 2048])
    c_smask = din("c_smask", [128, 17 * NS])
    c_caus = din("c_caus", [128, 128])
    c_keep = din("c_keep", [8, 1024])
    c_sel = din("c_sel", [8, 1024])

    yT = dout("yT", [D, S])
    ysT = dout("ysT", [D, NS])
    kT_out = dout("kT_out", [L, 2048, S])
    v_out = dout("v_out", [L, S, 2048])
    st_out = dout("st_out", [L, 8, 128, 257])
    m_out = dout("m_out", [L, 8, 1])
    ks_out = dout("ks_out", [L, 2048, NS])
    vs_out = dout("vs_out", [L, NS, 2048])
    sts_out = dout("sts_out", [L, 8, 128, 257])
    ms_out = dout("ms_out", [L, 8, 1])
    va_out = dout("va_out", [L, NS, 2048])
    KDBG = int(os.environ.get("KDBG", "0"))
    dbg = dout("dbg", [128, 1024]) if KDBG else None

    uT_d = dscr("uT_d", [2048, S])
    vaG_d = dscr("vaG_d", [S, 2048])
    qT_d = dscr("qT_d", [2048, S])
    kT_d = dscr("kT_d", [2048, S])
    V_d = dscr("V_d", [S, 2048])
    qcT_d = dscr("qcT_d", [1024, S])
    kcT_d = dscr("kcT_d", [1024, S])
    kc_d = dscr("kc_d", [S, 1024])
    vc_d = dscr("vc_d", [S, 2048])
    ogT_d = dscr("ogT_d", [2048, S])
    gT_d = dscr("gT_d", [3 * D, S])
    brT_d = [dscr("brT%d_d" % i, [2048, S]) for i in range(3)]
    vaGs_d = dscr("vaGs_d", [NS, 2048])
    Vs_d = dscr("Vs_d", [NS, 2048])
    kcs_d = dscr("kcs_d", [NS, 1024])
    vcs_d = dscr("vcs_d", [NS, 2048])

    yT_b = [[Buf() for _ in range(2)] for _ in range(32)]
    scr_b = {}

    def sbuf_of(key):
        if key not in scr_b:
            scr_b[key] = Buf()
        return scr_b[key]

    ws = WStream(sc, nc)
    ps = [nc.alloc_psum_tensor("ps%d" % i, [128, 512], F32) for i in range(8)]
    pb = bufs(8)
    ident_f = sb("ident_f", [128, 128], F32)
    ident_b = sb("ident_b", [128, 128], BF16)
    ones_b = sb("ones_b", [128, 128], BF16)
    amask = sb("amask", [128, 2048], BF16)
    smask = sb("smask", [128, 17 * NS], BF16)
    caus_f = sb("caus_f", [128, 128], F32)
    tmp_f = sb("tmp_f", [128, 512], F32)
    cb = Buf()
    Cst = sb("Cst", [128, 8 * 257], F32)
    Cbf = sb("Cbf", [128, 8 * 257], BF16)
    Csts = sb("Csts", [128, 8 * 257], F32)
    Cbfs = sb("Cbfs", [128, 8 * 257], BF16)
    Cst_b, Cbf_b, Csts_b, Cbfs_b = bufs(8), bufs(8), bufs(8), bufs(8)
    mst = sb("mst", [8, 1], F32)
    msts = sb("msts", [8, 1], F32)
    mst_b, msts_b = Buf(), Buf()
    xsT = sb("xsT_sb", [128, 32 * NS], F32)
    xsT_b = Buf()
    xnTs = sb("xnTs", [128, 32 * NS], BF16)
    xnTs_b = Buf()

    tb0 = Buf()
    sc.dma("sp", ident_f[:], c_ident[:, :], writes=[cb])
    sc.dma("sp", caus_f[:], c_caus[:, :], writes=[cb])
    for pc in range(4):
        sc.dma("sp", tmp_f[:], c_amask[:, pc * 512:(pc + 1) * 512], writes=[tb0])
        sc.op("dve", lambda e, pc=pc: e.tensor_copy(out=amask[:, pc * 512:(pc + 1) * 512], in_=tmp_f[:]),
              reads=[tb0], writes=[cb])
    sc.dma("sp", tmp_f[:, 0:17 * NS], c_smask[:, :], writes=[tb0])
    sc.op("dve", lambda e: e.tensor_copy(out=smask[:], in_=tmp_f[:, 0:17 * NS]), reads=[tb0], writes=[cb])
    sc.op("dve", lambda e: e.tensor_copy(out=ident_b[:], in_=ident_f[:]), reads=[cb], writes=[cb])
    sc.op("dve", lambda e: e.memset(ones_b[:], 1.0), writes=[cb])
    for fb in range(32):
        sc.dma("sp", yT[fb * 128:(fb + 1) * 128, :], xT_in[fb * 128:(fb + 1) * 128, :],
               writes=[yT_b[fb][0], yT_b[fb][1]])
    sc.dma("sp", xsT[:].rearrange("p (k t) -> p k t", k=32),
           xsT_in.rearrange("(k p) t -> p k t", p=128), writes=[xsT_b])

    state = dict(nc=nc, sc=sc, ws=ws, ps=ps, pb=pb)
    g = dict(locals())
    for l in range(L if KSTOP > 10 else (1 if KSTOP > 0 else 0)):
        emit_layer(g, l)
    sc.dma("sp", ysT.rearrange("(k p) t -> p k t", p=128),
           xsT[:].rearrange("p (k t) -> p k t", k=32), reads=[xsT_b])
    sc.finish()
    return nc


class NS_:
    pass


def win_tiles():
    T = []

    def add(sec, o, n, mode, step=256):
        for c in range(0, n, step):
            T.append((sec, o + c, min(step, n - c), mode, c))
    add("uA", O_UA, 2048, "fm")
    add("vA", O_VA, 2048, "tm")
    add("qB", O_QB, 2048, "fm")
    add("kB", O_KB, 2048, "fm")
    add("vB", O_VB, 2048, "tm")
    add("qC", O_QC, 1024, "fm")
    add("kC", O_KC, 1024, "fmtm")
    add("vC", O_VC, 2048, "tm")
    add("oC", O_OC, 2048, "fm")
    add("if", O_IC, 16, "if")
    add("g", O_G, 3 * D, "fm")
    return T


def emit_layer(g, l):
    G = NS_()
    G.__dict__.update(g)
    nc, sc, ws, ps, pb = G.nc, G.sc, G.ws, G.ps, G.pb
    G.bank_i = 0

    def sbt(es, name, shape, dt):
        return es.enter_context(nc.sbuf_tensor("%s_l%d_%d" % (name, l, sc.n_instr), list(shape), dt))

    def next_banks(n):
        i = G.bank_i
        G.bank_i += 1
        if n == 3:
            base = (i % 2) * 3
            return [base, base + 1, base + 2]
        return [i % 6]

    with contextlib.ExitStack() as LS:
        g1 = sbt(LS, "g1", [128, 32], F32)
        g2 = sbt(LS, "g2", [128, 32], F32)
        gq = sbt(LS, "gq", [128, 1], F32)
        gk = sbt(LS, "gk", [128, 1], F32)
        ibf = sbt(LS, "ibf", [8, 2], F32)
        igT = sbt(LS, "igT", [8, 1024], F32)
        lfT = sbt(LS, "lfT", [8, 1024], F32)
        igTs = sbt(LS, "igTs", [8, NS], F32)
        lfTs = sbt(LS, "lfTs", [8, NS], F32)
        ssv = sbt(LS, "ssv", [128, 8 * 8], F32)
        ssvs = sbt(LS, "ssvs", [128, 8], F32)
        uTs = sbt(LS, "uTs", [128, 16 * NS], BF16)
        qTs = sbt(LS, "qTs", [128, 16 * NS], BF16)
        kTs = sbt(LS, "kTs", [128, 16 * NS], BF16)
        qcTs = sbt(LS, "qcTs", [128, 8 * NS], BF16)
        kcTs = sbt(LS, "kcTs", [128, 8 * NS], BF16)
        ogTs = sbt(LS, "ogTs", [128, 16 * NS], BF16)
        gTs = sbt(LS, "gTs", [128, 96 * NS], BF16)
        brTs = [sbt(LS, "brTs%d" % i, [128, 16 * NS], BF16) for i in range(3)]
        lp = Buf()
        gate_b, gates_b, ssv_b, smp_b = Buf(), Buf(), Buf(), Buf()
        brTs_b = bufs(3)
        sc.dma("sp", g1[:], G.g1_in[l], writes=[lp])
        sc.dma("sp", g2[:], G.g2_in[l], writes=[lp])
        sc.dma("sp", gq[:], G.gq_in[l], writes=[lp])
        sc.dma("sp", gk[:], G.gk_in[l], writes=[lp])
        sc.dma("sp", ibf[:, 0:1], G.ib_in[l], writes=[lp])
        sc.dma("sp", ibf[:, 1:2], G.fb_in[l], writes=[lp])
        sc.op("dve", lambda e: e.tensor_scalar(out=gk[:], in0=gk[:], scalar1=float(np.sqrt(128.0)),
                                               scalar2=None, op0=ALU.mult), reads=[lp], writes=[lp])
        G.l = l
        G.lp = lp
        loc = dict(locals())
        for h in range(2 if KSTOP >= 8 else 1):
            emit_half(G, loc, l, h)
        sc.barrier()


def emit_half(G, loc, l, h):
    H = NS_()
    H.__dict__.update(loc)
    nc, sc, ws, ps, pb = G.nc, G.sc, G.ws, G.ps, G.pb
    tok0 = h * TH
    do_s = (h == 0)
    sbt = H.sbt
    next_banks = H.next_banks
    ident_b, ones_b = G.ident_b, G.ones_b
    cb = G.cb

    def kview(ap2d, k):
        return ap2d.rearrange("p (k t) -> p k t", k=k)

    def rmsnorm(es, gt, outT, outT_b, outTs):
        with contextlib.ExitStack() as st:
            stg = [sbt(st, "nstg%d" % i, [128, 8 * 512], F32) for i in range(2)]
            stg_b = bufs(2)
            sq = [sbt(st, "nsq%d" % i, [128, 512], BF16) for i in range(2)]
            sq_b = bufs(2)
            rs = sbt(st, "nrs", [128, 512], F32)
            rs_b = Buf()
            ld = 0
            for blk in range(2):
                t0 = tok0 + blk * 512
                for pas in range(2):
                    for grp in range(4):
                        i = ld % 2
                        ld += 1
                        src = G.yT[grp * 1024:(grp + 1) * 1024, t0:t0 + 512].rearrange("(k p) t -> p k t", p=128)
                        sc.dma("sp", kview(stg[i][:], 8), src,
                               reads=[G.yT_b[grp * 8 + j][h] for j in range(8)], writes=[stg_b[i]])
                        for j in range(8):
                            kc = grp * 8 + j
                            xin = stg[i][:, j * 512:(j + 1) * 512]
                            if pas == 0:
                                s_ = kc % 2
                                sc.op("act", lambda e, o=sq[s_], x=xin: e.activation(out=o[:], in_=x, func=AF.Square),
                                      reads=[stg_b[i]], writes=[sq_b[s_]])
                                sc.op("pe", lambda e, o=sq[s_], kc=kc: e.matmul(ps[6][:], ones_b[:], o[:],
                                                                                  start=(kc == 0), stop=(kc == 31)),
                                      reads=[sq_b[s_], cb], writes=[pb[6]])
                            else:
                                sc.op("dve", lambda e, x=xin, kc=kc, blk=blk: e.scalar_tensor_tensor(
                                    out=outT[:, kc * 1024 + blk * 512: kc * 1024 + (blk + 1) * 512], in0=x,
                                    scalar=gt[:, kc:kc + 1], in1=rs[:], op0=ALU.mult, op1=ALU.mult),
                                    reads=[stg_b[i], rs_b, G.lp], writes=[outT_b[blk]])
                    if pas == 0:
                        sc.op("act", lambda e: e.activation(out=rs[:], in_=ps[6][:], func=AF.Sqrt,
                                                            bias=EPS, scale=1.0 / D),
                              reads=[pb[6]], writes=[rs_b])
                        sc.op("dve", lambda e: e.reciprocal(out=rs[:], in_=rs[:]), reads=[rs_b], writes=[rs_b])
            if do_s:
                sqs = sbt(st, "nsqs", [128, 32 * NS], BF16)
                rss = sbt(st, "nrss", [128, NS], F32)
                tmps = sbt(st, "ntmps", [128, 32 * NS], F32)
                b1 = Buf()
                sc.op("act", lambda e: e.activation(out=sqs[:], in_=G.xsT[:], func=AF.Square),
                      reads=[G.xsT_b], writes=[b1])
                for kc in range(32):
                    sc.op("pe", lambda e, kc=kc: e.matmul(ps[7][:, 0:NS], ones_b[:], sqs[:, kc * NS:(kc + 1) * NS],
                                                          start=(kc == 0), stop=(kc == 31)),
                          reads=[b1, cb], writes=[pb[7]], signal=(kc == 31))
                sc.op("act", lambda e: e.activation(out=rss[:], in_=ps[7][:, 0:NS], func=AF.Sqrt,
                                                    bias=EPS, scale=1.0 / D), reads=[pb[7]], writes=[b1])
                sc.op("dve", lambda e: e.reciprocal(out=rss[:], in_=rss[:]), reads=[b1], writes=[b1])
                sc.op("dve", lambda e: e.tensor_tensor(
                    out=kview(tmps[:], 32), in0=kview(G.xsT[:], 32),
                    in1=gt[:, :].unsqueeze(2).broadcast_to([128, 32, NS]), op=ALU.mult),
                    reads=[G.xsT_b, G.lp, b1], writes=[b1])
                sc.op("dve", lambda e: e.tensor_tensor(
                    out=kview(outTs[:], 32), in0=kview(tmps[:], 32),
                    in1=rss[:, :].unsqueeze(1).broadcast_to([128, 32, NS]), op=ALU.mult),
                    reads=[b1], writes=[G.xnTs_b])
        sc.barrier()

    H.kview = kview
    H.rmsnorm = rmsnorm
    H.gemm = lambda *a, **k: gemm(G, H, *a, **k)
    H.tok0 = tok0
    H.do_s = do_s
    H.h = h
    stage_win(G, H)
    if KSTOP <= 2:
        return
    stage_gmlp(G, H)
    if KSTOP <= 3:
        return
    stage_attn(G, H)
    if KSTOP <= 4:
        return
    stage_mlstm(G, H)
    if G.dbg is not None and h == 0 and l == 0:
        with contextlib.ExitStack() as es_:
            dt_ = sbt(es_, "dbgt", [128, 1024], F32)
            db_ = Buf()
            sc.op("dve", lambda e: e.memset(dt_[:], 0.0), writes=[db_])
            for i_ in range(3):
                sc.op("dve", lambda e, i_=i_: e.tensor_copy(out=dt_[:, i_ * 64:(i_ + 1) * 64], in_=H.brTs[i_][:]),
                      reads=[H.brTs_b[i_]], writes=[db_])
            sc.op("dve", lambda e: e.tensor_copy(out=dt_[:, 192:192 + 96 * NS], in_=H.gTs[:]), reads=[H.smp_b], writes=[db_])
            sc.dma("sp", G.dbg[:, :], dt_[:], reads=[db_])
        sc.barrier()
    if KSTOP <= 5:
        return
    stage_merge(G, H)
    if KSTOP <= 6:
        return
    stage_ffn(G, H)


def gemm(G, H, es, tiles, wsrc_of, kcn, actT, actT_b, actTs, actTs_b, epi_fm, epi_tm, epi_if=None):
    sc, ws, ps, pb = G.sc, G.ws, G.ps, G.pb
    next_banks, do_s = H.next_banks, H.do_s
    ws.plan.extend([(wsrc_of(c0, ncol), kcn, ncol, 128) for (sec, c0, ncol, mode, crel) in tiles])
    for (sec, c0, ncol, mode, crel) in tiles:
        wv, wb = ws.next(None)
        if "fm" in mode:
            for cbk in range((ncol + 127) // 128):
                m = min(128, ncol - cbk * 128)
                bA, bB, bS = next_banks(3)
                for kc in range(kcn):
                    lhsT = wv[:, kc, cbk * 128:cbk * 128 + m]
                    last = (kc == kcn - 1)
                    for bk, blk in ((bA, 0), (bB, 1)):
                        sc.op("pe", lambda e, bk=bk, blk=blk, lhsT=lhsT, kc=kc: e.matmul(
                            ps[bk][0:m, :], lhsT, actT[:, kc * 1024 + blk * 512: kc * 1024 + (blk + 1) * 512],
                            start=(kc == 0), stop=(kc == kcn - 1)),
                            reads=[wb, actT_b[blk]], writes=[pb[bk]], signal=last)
                    if do_s:
                        sc.op("pe", lambda e, lhsT=lhsT, kc=kc: e.matmul(
                            ps[bS][0:m, 0:NS], lhsT, actTs[:, kc * NS:(kc + 1) * NS],
                            start=(kc == 0), stop=(kc == kcn - 1)),
                            reads=[wb, actTs_b], writes=[pb[bS]], signal=last)
                epi_fm(sec, crel + cbk * 128, m, bA, bB, bS)
        if "tm" in mode:
            for st_ in range(NSUB):
                (bk,) = next_banks(1)
                for kc in range(kcn):
                    sc.op("pe", lambda e, bk=bk, kc=kc, st_=st_: e.matmul(
                        ps[bk][:, 0:ncol], actT[:, kc * 1024 + st_ * 128: kc * 1024 + (st_ + 1) * 128],
                        wv[:, kc, 0:ncol], start=(kc == 0), stop=(kc == kcn - 1)),
                        reads=[wb, actT_b[st_ // 4]], writes=[pb[bk]], signal=(kc == kcn - 1))
                epi_tm(sec, crel, ncol, st_, bk, False)
            if do_s:
                (bk,) = next_banks(1)
                for kc in range(kcn):
                    sc.op("pe", lambda e, bk=bk, kc=kc: e.matmul(
                        ps[bk][0:NS, 0:ncol], actTs[:, kc * NS:(kc + 1) * NS], wv[:, kc, 0:ncol],
                        start=(kc == 0), stop=(kc == kcn - 1)),
                        reads=[wb, actTs_b], writes=[pb[bk]], signal=(kc == kcn - 1))
                epi_tm(sec, crel, ncol, 0, bk, True)
        if mode == "if":
            for which in range(2):
                bA, bB, bS = next_banks(3)
                for kc in range(kcn):
                    lhsT = wv[:, kc, which * 8:(which + 1) * 8]
                    last = (kc == kcn - 1)
                    for bk, blk in ((bA, 0), (bB, 1)):
                        sc.op("pe", lambda e, bk=bk, blk=blk, lhsT=lhsT, kc=kc: e.matmul(
                            ps[bk][0:8, :], lhsT, actT[:, kc * 1024 + blk * 512: kc * 1024 + (blk + 1) * 512],
                            start=(kc == 0), stop=(kc == kcn - 1)),
                            reads=[wb, actT_b[blk]], writes=[pb[bk]], signal=last)
                    if do_s:
                        sc.op("pe", lambda e, lhsT=lhsT, kc=kc: e.matmul(
                            ps[bS][0:8, 0:NS], lhsT, actTs[:, kc * NS:(kc + 1) * NS],
                            start=(kc == 0), stop=(kc == kcn - 1)),
                            reads=[wb, actTs_b], writes=[pb[bS]], signal=last)
                epi_if(which, bA, bB, bS)


def stage_win(G, H):
    nc, sc, ws, ps, pb = G.nc, G.sc, G.ws, G.ps, G.pb
    l, h, tok0, do_s, sbt = G.l, H.h, H.tok0, H.do_s, H.sbt
    ones_b, cb = G.ones_b, G.cb
    with contextlib.ExitStack() as es:
        xnT = sbt(es, "xnT", [128, 32 * 1024], BF16)
        xnT_b = bufs(2)
        H.rmsnorm(es, H.g1, xnT, xnT_b, G.xnTs)
        if KSTOP <= 1:
            return
        stA = [sbt(es, "stA%d" % i, [128, 1024], BF16) for i in range(3)]
        stA_b = bufs(3)
        raw = [sbt(es, "raw%d" % i, [128, 1024], F32) for i in range(2)]
        raw_b = bufs(2)
        sqh = sbt(es, "sqh", [128, 1024], BF16)
        sqh_b = Buf()
        rsh = sbt(es, "rsh", [128, 1024], F32)
        rsh_b = Buf()
        tmb = [sbt(es, "tmb%d" % i, [128, 256], BF16) for i in range(3)]
        tmb_b = bufs(3)
        tmf = [sbt(es, "tmf%d" % i, [128, 256], F32) for i in range(2)]
        tmf_b = bufs(2)
        junk = sbt(es, "junk", [128, 256], BF16)
        junk_b = Buf()
        sm32 = sbt(es, "sm32", [128, 8 * NS], F32)
        sm_b = Buf()
        nfb = sbt(es, "nfb", [8, 1], F32)
        ift = sbt(es, "ift", [8, 512], F32)
        ift_b = Buf()
        sc.op("dve", lambda e: e.tensor_scalar(out=nfb[:], in0=H.ibf[:, 1:2], scalar1=-1.0, scalar2=None,
                                               op0=ALU.mult), reads=[G.lp], writes=[ift_b])
        cnt = dict(a=0, r=0, tb=0, tf=0)

        def fm_store(func, bA, bB, dst, scale=1.0):
            i = cnt["a"] % 3
            cnt["a"] += 1
            for bk, blk in ((bA, 0), (bB, 1)):
                sc.op("act", lambda e, bk=bk, blk=blk: e.activation(
                    out=stA[i][0:128, blk * 512:(blk + 1) * 512], in_=ps[bk][:, :], func=func, scale=scale),
                    reads=[pb[bk]], writes=[stA_b[i]])
            sc.dma("sp", dst, stA[i][:, :], reads=[stA_b[i]], writes=[])

        def headnorm(is_k, f, bA, bB, bS):
            r = cnt["r"] % 2
            cnt["r"] += 1
            gsc = H.gk if is_k else H.gq
            for bk, blk in ((bA, 0), (bB, 1)):
                sl = slice(blk * 512, (blk + 1) * 512)
                sc.op("act", lambda e, bk=bk, sl=sl: e.activation(out=sqh[:, sl], in_=ps[bk][:, :], func=AF.Square),
                      writes=[sqh_b, pb[bk]])
                sc.op("dve", lambda e, bk=bk, sl=sl: e.tensor_copy(out=raw[r][:, sl], in_=ps[bk][:, :]),
                      writes=[raw_b[r], pb[bk]])
            for blk in range(2):
                sl = slice(blk * 512, (blk + 1) * 512)
                sc.op("pe", lambda e, blk=blk, sl=sl: e.matmul(ps[6 + blk][:, :], ones_b[:], sqh[:, sl],
                                                             start=True, stop=True),
                      reads=[sqh_b, cb], writes=[pb[6 + blk]])
                sc.op("act", lambda e, blk=blk, sl=sl: e.activation(out=rsh[:, sl], in_=ps[6 + blk][:, :],
                                                                  func=AF.Sqrt, bias=128.0 * EPS, scale=1.0),
                      reads=[pb[6 + blk]], writes=[rsh_b])
            sc.op("dve", lambda e: e.reciprocal(out=rsh[:], in_=rsh[:]), reads=[rsh_b], writes=[rsh_b])
            i = cnt["a"] % 3
            cnt["a"] += 1
            if is_k:
                sc.op("dve", lambda e: e.scalar_tensor_tensor(out=raw[r][:], in0=raw[r][:], scalar=gsc[:, 0:1],
                                                              in1=rsh[:], op0=ALU.mult, op1=ALU.mult),
                      reads=[raw_b[r], rsh_b, G.lp], writes=[raw_b[r]])
                sc.dma("sp", G.kT_out[l, f:f + 128, tok0:tok0 + TH], raw[r][:], reads=[raw_b[r]])
                sc.op("act", lambda e: e.activation(out=stA[i][:], in_=raw[r][:], func=AF.Copy),
                      reads=[raw_b[r]], writes=[stA_b[i]])
                sc.dma("sp", G.kT_d[f:f + 128, tok0:tok0 + TH], stA[i][:], reads=[stA_b[i]])
            else:
                sc.op("dve", lambda e: e.scalar_tensor_tensor(out=stA[i][:], in0=raw[r][:], scalar=gsc[:, 0:1],
                                                              in1=rsh[:], op0=ALU.mult, op1=ALU.mult),
                      reads=[raw_b[r], rsh_b, G.lp], writes=[stA_b[i]])
                sc.dma("sp", G.qT_d[f:f + 128, tok0:tok0 + TH], stA[i][:], reads=[stA_b[i]])
            if do_s:
                hd = f // 128
                sc.op("act", lambda e: e.activation(out=sqh[:, 0:NS], in_=ps[bS][:, 0:NS], func=AF.Square),
                      writes=[sqh_b, pb[bS]])
                sc.op("dve", lambda e: e.tensor_copy(out=sm32[:, 0:NS], in_=ps[bS][:, 0:NS]),
                      writes=[sm_b, pb[bS]])
                sc.op("pe", lambda e: e.matmul(ps[6][:, 0:NS], ones_b[:], sqh[:, 0:NS], start=True, stop=True),
                      reads=[sqh_b, cb], writes=[pb[6]])
                sc.op("act", lambda e: e.activation(out=sm32[:, NS:2 * NS], in_=ps[6][:, 0:NS], func=AF.Sqrt,
                                                    bias=128.0 * EPS, scale=1.0), reads=[pb[6]], writes=[sm_b])
                sc.op("dve", lambda e: e.reciprocal(out=sm32[:, NS:2 * NS], in_=sm32[:, NS:2 * NS]),
                      reads=[sm_b], writes=[sm_b])
                sc.op("dve", lambda e: e.scalar_tensor_tensor(out=sm32[:, 2 * NS:3 * NS], in0=sm32[:, 0:NS],
                                                              scalar=gsc[:, 0:1], in1=sm32[:, NS:2 * NS],
                                                              op0=ALU.mult, op1=ALU.mult),
                      reads=[sm_b, G.lp], writes=[sm_b])
                dstT = H.kTs if is_k else H.qTs
                sc.op("act", lambda e: e.activation(out=dstT[:, hd * NS:(hd + 1) * NS], in_=sm32[:, 2 * NS:3 * NS],
                                                    func=AF.Copy), reads=[sm_b], writes=[H.smp_b])
                if is_k:
                    sc.dma("sp", G.ks_out[l, f:f + 128, :], sm32[:, 2 * NS:3 * NS], reads=[sm_b])

        def epi_fm(sec, f, m, bA, bB, bS):
            fb = f // 128
            if sec == "uA":
                fm_store(AF.Gelu, bA, bB, G.uT_d[f:f + 128, tok0:tok0 + TH])
                sfun, sdst, sscale = AF.Gelu, H.uTs, 1.0
            elif sec == "qB":
                headnorm(False, f, bA, bB, bS)
                return
            elif sec == "kB":
                headnorm(True, f, bA, bB, bS)
                return
            elif sec == "qC":
                fm_store(AF.Copy, bA, bB, G.qcT_d[f:f + 128, tok0:tok0 + TH])
                sfun, sdst, sscale = AF.Copy, H.qcTs, 1.0
            elif sec == "kC":
                fm_store(AF.Copy, bA, bB, G.kcT_d[f:f + 128, tok0:tok0 + TH], scale=128.0 ** -0.5)
                sfun, sdst, sscale = AF.Copy, H.kcTs, 128.0 ** -0.5
            elif sec == "oC":
                fm_store(AF.Sigmoid, bA, bB, G.ogT_d[f:f + 128, tok0:tok0 + TH])
                sfun, sdst, sscale = AF.Sigmoid, H.ogTs, 1.0
            else:
                fm_store(AF.Sigmoid, bA, bB, G.gT_d[f:f + 128, tok0:tok0 + TH])
                sfun, sdst, sscale = AF.Sigmoid, H.gTs, 1.0
            if do_s:
                sc.op("act", lambda e: e.activation(out=sdst[:, fb * NS:(fb + 1) * NS], in_=ps[bS][:, 0:NS],
                                                    func=sfun, scale=sscale), reads=[pb[bS]], writes=[H.smp_b])

        def epi_tm(sec, crel, ncol, st_, bk, smp):
            R = NS if smp else 128
            r0 = 0 if smp else tok0 + st_ * 128
            i = cnt["tb"] % 3
            cnt["tb"] += 1
            if sec == "vA":
                sc.op("act", lambda e: e.activation(out=tmb[i][0:R, :], in_=ps[bk][0:R, 0:256], func=AF.Gelu),
                      reads=[pb[bk]], writes=[tmb_b[i]])
                acc = (H.ssvs[0:R, crel // 256: crel // 256 + 1] if smp
                       else H.ssv[:, st_ * 8 + crel // 256: st_ * 8 + crel // 256 + 1])
                sc.op("act", lambda e: e.activation(out=junk[0:R, :], in_=tmb[i][0:R, :], func=AF.Square,
                                                    accum_out=acc),
                      reads=[tmb_b[i]], writes=[junk_b, H.ssv_b])
                dst = G.vaGs_d if smp else G.vaG_d
                sc.dma("sp", dst[r0:r0 + R, crel:crel + 256], tmb[i][0:R, :], reads=[tmb_b[i]])
            elif sec == "vB":
                j = cnt["tf"] % 2
                cnt["tf"] += 1
                sc.op("dve", lambda e: e.tensor_copy(out=tmf[j][0:R, :], in_=ps[bk][0:R, 0:256]),
                      writes=[tmf_b[j], pb[bk]])
                sc.op("act", lambda e: e.activation(out=tmb[i][0:R, :], in_=tmf[j][0:R, :], func=AF.Copy),
                      reads=[tmf_b[j]], writes=[tmb_b[i]])
                d32 = G.vs_out[l] if smp else G.v_out[l]
                d16 = G.Vs_d if smp else G.V_d
                sc.dma("sp", d32[r0:r0 + R, crel:crel + 256], tmf[j][0:R, :], reads=[tmf_b[j]])
                sc.dma("sp", d16[r0:r0 + R, crel:crel + 256], tmb[i][0:R, :], reads=[tmb_b[i]])
            else:
                scale = 128.0 ** -0.5 if sec == "kC" else 1.0
                sc.op("act", lambda e: e.activation(out=tmb[i][0:R, :], in_=ps[bk][0:R, 0:256], func=AF.Copy,
                                                    scale=scale), reads=[pb[bk]], writes=[tmb_b[i]])
                if sec == "kC":
                    dst = G.kcs_d if smp else G.kc_d
                else:
                    dst = G.vcs_d if smp else G.vc_d
                sc.dma("sp", dst[r0:r0 + R, crel:crel + 256], tmb[i][0:R, :], reads=[tmb_b[i]])

        def epi_if(which, bA, bB, bS):
            srcs = [(bA, H.igT if which == 0 else H.lfT, slice(0, 512), 512),
                    (bB, H.igT if which == 0 else H.lfT, slice(512, 1024), 512)]
            if do_s:
                srcs.append((bS, H.igTs if which == 0 else H.lfTs, slice(0, NS), NS))
            for bk, dst, sl, n in srcs:
                if which == 0:
                    sc.op("act", lambda e, bk=bk, dst=dst, sl=sl, n=n: e.activation(
                        out=dst[:, sl], in_=ps[bk][0:8, 0:n], func=AF.Identity, bias=H.ibf[:, 0:1]),
                        reads=[pb[bk], G.lp], writes=[H.gate_b])
                else:
                    sc.op("act", lambda e, bk=bk, n=n: e.activation(
                        out=ift[:, 0:n], in_=ps[bk][0:8, 0:n], func=AF.Exp, bias=nfb[:, 0:1], scale=-1.0),
                        reads=[pb[bk], ift_b], writes=[ift_b])
                    sc.op("act", lambda e, n=n: e.activation(out=ift[:, 0:n], in_=ift[:, 0:n], func=AF.Ln, bias=1.0),
                          reads=[ift_b], writes=[ift_b])
                    sc.op("dve", lambda e, dst=dst, sl=sl, n=n: e.tensor_scalar(
                        out=dst[:, sl], in0=ift[:, 0:n], scalar1=-1.0, scalar2=None, op0=ALU.mult),
                        reads=[ift_b], writes=[H.gate_b])

        wl = G.w_in[l].rearrange("(k p) n -> p k n", p=128)
        H.gemm(es, win_tiles()[:KTILES], lambda c0, n: wl[:, :, c0:c0 + n], 32, xnT, xnT_b, G.xnTs, G.xnTs_b,
               epi_fm, epi_tm, epi_if)
    sc.barrier()


def stage_gmlp(G, H):
    nc, sc, ws, ps, pb = G.nc, G.sc, G.ws, G.ps, G.pb
    l, h, tok0, do_s, sbt = G.l, H.h, H.tok0, H.do_s, H.sbt
    with contextlib.ExitStack() as es:
        gva = sbt(es, "gva", [128, 2048], F32)
        wT_f = sbt(es, "wT_f", [128, 1024], F32)
        wT = sbt(es, "wT", [128, 1024], BF16)
        bT = sbt(es, "bT", [128, 1024], F32)
        rst = sbt(es, "rst", [128, 8], F32)
        cst_b = Buf()
        sc.dma("sp", gva[:], G.gva_in[l], writes=[cst_b])
        sc.dma("sp", wT_f[:], G.wsT_in[l], writes=[cst_b])
        sc.dma("sp", bT[:], G.bT_in[l], writes=[cst_b])
        for g_ in range(8):
            sc.op("dve", lambda e, g_=g_: e.tensor_tensor(out=wT[:, g_ * 128:(g_ + 1) * 128],
                                                         in0=wT_f[:, g_ * 128:(g_ + 1) * 128], in1=G.caus_f[:],
                                                         op=ALU.mult), reads=[cst_b, G.cb], writes=[cst_b])
        sc.op("dve", lambda e: e.tensor_reduce(out=rst[:], in_=H.ssv[:].rearrange("p (s c) -> p s c", c=8),
                                               axis=AX.X, op=ALU.add), reads=[H.ssv_b], writes=[cst_b])
        sc.op("act", lambda e: e.activation(out=rst[:], in_=rst[:], func=AF.Sqrt, bias=EPS, scale=1.0 / 2048),
              reads=[cst_b], writes=[cst_b])
        sc.op("dve", lambda e: e.reciprocal(out=rst[:], in_=rst[:]), reads=[cst_b], writes=[cst_b])
        va = [sbt(es, "va%d" % i, [128, 2048], BF16) for i in range(2)]
        va_b = bufs(2)
        van = [sbt(es, "van%d" % i, [128, 2048], BF16) for i in range(2)]
        van_b = bufs(2)
        uTc = [sbt(es, "uTc%d" % i, [128, 2048], BF16) for i in range(2)]
        uTc_b = bufs(2)
        ao = [sbt(es, "ao%d" % i, [128, 2048], BF16) for i in range(2)]
        ao_b = bufs(2)
        tmp = sbt(es, "gtmp", [128, 256], F32)
        tmp_b = Buf()
        for st_ in range(NSUB):
            i = st_ % 2
            t0 = tok0 + st_ * 128
            sc.dma("sp", va[i][:], G.vaG_d[t0:t0 + 128, :], writes=[va_b[i]])
            sc.dma("sp", uTc[i][:].rearrange("p (k t) -> p k t", k=16),
                   G.uT_d[:, t0:t0 + 128].rearrange("(k p) t -> p k t", p=128), writes=[uTc_b[i]])
            sc.op("dve", lambda e, i=i, st_=st_: e.scalar_tensor_tensor(
                out=van[i][:], in0=va[i][:], scalar=rst[:, st_:st_ + 1], in1=gva[:], op0=ALU.mult, op1=ALU.mult),
                reads=[va_b[i], cst_b], writes=[van_b[i]])
            for q in range(4):
                (bk,) = H.next_banks(1)
                for j in range(4):
                    fb = q * 4 + j
                    g_ = fb // 2
                    sc.op("pe", lambda e, bk=bk, j=j, fb=fb, g_=g_, i=i: e.matmul(
                        ps[bk][:, j * 128:(j + 1) * 128], van[i][:, fb * 128:(fb + 1) * 128],
                        wT[:, g_ * 128:(g_ + 1) * 128], start=True, stop=True),
                        reads=[van_b[i], cst_b], writes=[pb[bk]])
                for gg in range(2):
                    g_ = q * 2 + gg
                    fb0 = q * 4 + gg * 2
                    sc.op("dve", lambda e, bk=bk, gg=gg, g_=g_: e.tensor_tensor(
                        out=tmp[:].rearrange("p (a t) -> p a t", a=2),
                        in0=ps[bk][:, gg * 256:(gg + 1) * 256].rearrange("p (a t) -> p a t", a=2),
                        in1=bT[:, g_ * 128:(g_ + 1) * 128].unsqueeze(1).broadcast_to([128, 2, 128]), op=ALU.add),
                        reads=[pb[bk], cst_b], writes=[tmp_b])
                    sc.op("dve", lambda e, fb0=fb0, i=i: e.tensor_tensor(
                        out=ao[i][:, fb0 * 128:(fb0 + 2) * 128], in0=tmp[:], in1=uTc[i][:, fb0 * 128:(fb0 + 2) * 128],
                        op=ALU.mult), reads=[tmp_b, uTc_b[i]], writes=[ao_b[i]])
            sc.dma("sp", G.brT_d[0][:, t0:t0 + 128].rearrange("(k p) t -> p k t", p=128),
                   ao[i][:].rearrange("p (k t) -> p k t", k=16), reads=[ao_b[i]])
        if do_s:
            vas = sbt(es, "vas", [NS, 2048], BF16)
            vaf = sbt(es, "vaf", [NS, 2048], F32)
            rss = sbt(es, "grss", [NS, 1], F32)
            sb_ = Buf()
            sc.dma("sp", vas[:], G.vaGs_d[:, :], writes=[sb_])
            sc.op("dve", lambda e: e.tensor_reduce(out=rss[:], in_=H.ssvs[0:NS, 0:8], axis=AX.X, op=ALU.add),
                  reads=[H.ssv_b], writes=[sb_])
            sc.op("act", lambda e: e.activation(out=rss[:], in_=rss[:], func=AF.Sqrt, bias=EPS, scale=1.0 / 2048),
                  reads=[sb_], writes=[sb_])
            sc.op("dve", lambda e: e.reciprocal(out=rss[:], in_=rss[:]), reads=[sb_], writes=[sb_])
            sc.op("dve", lambda e: e.scalar_tensor_tensor(out=vaf[:], in0=vas[:], scalar=rss[:, 0:1], in1=gva[0:NS, :],
                                                          op0=ALU.mult, op1=ALU.mult),
                  reads=[sb_, cst_b], writes=[sb_])
            sc.dma("sp", G.va_out[l], vaf[:], reads=[sb_])
            sc.op("act", lambda e: e.activation(out=vas[:], in_=vaf[:], func=AF.Copy), reads=[sb_], writes=[sb_])
            (bk,) = H.next_banks(1)
            for fb in range(16):
                g_ = fb // 2
                sc.op("pe", lambda e, fb=fb, g_=g_: e.matmul(
                    ps[bk][:, fb * NS:(fb + 1) * NS], vas[0:NS, fb * 128:(fb + 1) * 128],
                    wT[0:NS, g_ * 128:g_ * 128 + NS], start=True, stop=True),
                    reads=[sb_, cst_b], writes=[pb[bk]])
            for fb in range(16):
                g_ = fb // 2
                sc.op("dve", lambda e, fb=fb, g_=g_: e.tensor_tensor(
                    out=tmp[:, 0:NS], in0=ps[bk][:, fb * NS:(fb + 1) * NS], in1=bT[:, g_ * 128:g_ * 128 + NS],
                    op=ALU.add), reads=[pb[bk], cst_b], writes=[tmp_b])
                sc.op("dve", lambda e, fb=fb: e.tensor_tensor(
                    out=H.brTs[0][:, fb * NS:(fb + 1) * NS], in0=tmp[:, 0:NS], in1=H.uTs[:, fb * NS:(fb + 1) * NS],
                    op=ALU.mult), reads=[tmp_b, H.smp_b], writes=[H.brTs_b[0]])
    sc.barrier()


def stage_attn(G, H):
    nc, sc, ws, ps, pb = G.nc, G.sc, G.ws, G.ps, G.pb
    l, h, tok0, do_s, sbt = G.l, H.h, H.tok0, H.do_s, H.sbt
    ones_b, ident_b, amask, smask, cb = G.ones_b, G.ident_b, G.amask, G.smask, G.cb
    with contextlib.ExitStack() as es:
        nk = tok0 + TH
        nkb = nk // 128
        kT = [sbt(es, "akT%d" % i, [128, 2048], BF16) for i in range(2)]
        V = [sbt(es, "aV%d" % i, [128, 2048], BF16) for i in range(2)]
        qT = [sbt(es, "aqT%d" % i, [128, 1024], BF16) for i in range(2)]
        in_b = bufs(2)
        pt = [sbt(es, "apt%d" % i, [128, 512], BF16) for i in range(3)]
        pt_b = bufs(3)
        ob = [sbt(es, "aob%d" % i, [128, 1024], BF16) for i in range(2)]
        ob_b = bufs(2)
        rz = sbt(es, "arz", [128, 512], F32)
        rz_b = Buf()
        it = 0
        for hd in range(16):
            i = hd % 2
            hs = slice(hd * 128, (hd + 1) * 128)
            sc.dma("sp", kT[i][:, 0:nk], G.kT_d[hs, 0:nk], writes=[in_b[i]])
            sc.dma("sp", V[i][:, 0:nkb * 128].rearrange("p (k d) -> p k d", k=nkb),
                   G.V_d[0:nk, hs].rearrange("(k p) d -> p k d", p=128), writes=[in_b[i]])
            sc.dma("sp", qT[i][:], G.qT_d[hs, tok0:tok0 + TH], writes=[in_b[i]])
            for qb in range(2):
                qs = (tok0 + qb * 512) // 128
                nj = qs + 4
                bO, bZ = (3, 4) if it % 2 == 0 else (5, 6)
                it += 1

                def smm(j):
                    c0 = max(0, j - qs) * 128
                    sc.op("pe", lambda e: e.matmul(ps[j % 3][:, c0:512], kT[i][:, j * 128:(j + 1) * 128],
                                                   qT[i][:, qb * 512 + c0:(qb + 1) * 512], start=True, stop=True),
                          reads=[in_b[i]], writes=[pb[j % 3]])
                smm(0)
                for j in range(nj):
                    if j + 1 < nj:
                        smm(j + 1)
                    i0 = max(0, j - qs)
                    c0 = i0 * 128
                    dl0 = qs + i0 - j
                    pi = j % 3
                    sc.op("act", lambda e, j=j, c0=c0, pi=pi: e.activation(out=pt[pi][:, c0:512], in_=ps[j % 3][:, c0:512],
                                                                          func=AF.Exp),
                          reads=[pb[j % 3]], writes=[pt_b[pi]])
                    sc.op("dve", lambda e, c0=c0, pi=pi, dl0=dl0: e.tensor_tensor(
                        out=pt[pi][:, c0:512], in0=pt[pi][:, c0:512], in1=amask[:, dl0 * 128: dl0 * 128 + 512 - c0],
                        op=ALU.mult), reads=[pt_b[pi], cb], writes=[pt_b[pi]])
                    last = (j == nj - 1)
                    sc.op("pe", lambda e, j=j, c0=c0, pi=pi: e.matmul(ps[bO][:, c0:512], V[i][:, j * 128:(j + 1) * 128],
                                                                     pt[pi][:, c0:512], start=(j == 0), stop=(j == nj - 1)),
                          reads=[pt_b[pi], in_b[i]], writes=[pb[bO]], signal=last)
                    sc.op("pe", lambda e, j=j, c0=c0, pi=pi: e.matmul(ps[bZ][:, c0:512], ones_b[:], pt[pi][:, c0:512],
                                                                     start=(j == 0), stop=(j == nj - 1)),
                          reads=[pt_b[pi], cb], writes=[pb[bZ]], signal=True)
                sc.op("dve", lambda e: e.reciprocal(out=rz[:], in_=ps[bZ][:, :]), reads=[pb[bZ]], writes=[rz_b])
                sc.op("dve", lambda e: e.tensor_tensor(out=ob[i][:, qb * 512:(qb + 1) * 512], in0=ps[bO][:, :], in1=rz[:],
                                                       op=ALU.mult), reads=[pb[bO], rz_b], writes=[ob_b[i]])
            sc.dma("sp", G.brT_d[1][hs, tok0:tok0 + TH], ob[i][:], reads=[ob_b[i]])
        if do_s:
            Kc = sbt(es, "sKc", [128, 16 * 512], BF16)
            Vc = sbt(es, "sVc", [128, 16 * 512], BF16)
            c_b = Buf()
            Vs_sb = sbt(es, "sVs", [NS, 2048], BF16)
            vs_b = Buf()
            KT = sbt(es, "sKT", [128, 2048], BF16)
            KT_b = Buf()
            P = sbt(es, "sP", [128, 17 * NS], BF16)
            P_b = Buf()
            rzs = sbt(es, "srz", [128, NS], F32)
            sc.dma("sp", Vs_sb[:], G.Vs_d[:, :], writes=[vs_b])
            for hg in range(4):
                sc.dma("pool", Kc[:].rearrange("p (k c) -> p k c", k=16),
                       G.ck_in[l][:, hg * 512:(hg + 1) * 512].rearrange("(k p) c -> p k c", p=128), writes=[c_b],
                       after_barrier=True)
                sc.dma("pool", Vc[:].rearrange("p (k c) -> p k c", k=16),
                       G.cv_in[l][:, hg * 512:(hg + 1) * 512].rearrange("(k p) c -> p k c", p=128), writes=[c_b],
                       after_barrier=True)
                for hh in range(4):
                    hd = hg * 4 + hh
                    for k4 in range(4):
                        bk = k4 % 3
                        for j in range(4):
                            kb = k4 * 4 + j
                            sc.op("pe", lambda e, bk=bk, j=j, kb=kb: e.matmul(
                                ps[bk][:, j * 128:(j + 1) * 128], Kc[:, kb * 512 + hh * 128: kb * 512 + (hh + 1) * 128],
                                ident_b[:], start=True, stop=True), reads=[c_b, cb], writes=[pb[bk]])
                        sc.op("act", lambda e, bk=bk, k4=k4: e.activation(out=KT[:, k4 * 512:(k4 + 1) * 512],
                                                                          in_=ps[bk][:, :], func=AF.Copy),
                              reads=[pb[bk]], writes=[KT_b])
                    qs_ = H.qTs[:, hd * NS:(hd + 1) * NS]
                    for kb in range(16):
                        sc.op("pe", lambda e, kb=kb: e.matmul(ps[3][:, kb * NS:(kb + 1) * NS], KT[:, kb * 128:(kb + 1) * 128],
                                                              qs_, start=True, stop=True),
                              reads=[KT_b, H.smp_b], writes=[pb[3]])
                    sc.op("pe", lambda e: e.matmul(ps[3][0:NS, 16 * NS:17 * NS], H.kTs[:, hd * NS:(hd + 1) * NS], qs_,
                                                   start=True, stop=True), reads=[H.smp_b], writes=[pb[3]])
                    sc.op("act", lambda e: e.activation(out=P[:, 0:16 * NS], in_=ps[3][:, 0:16 * NS], func=AF.Exp),
                          reads=[pb[3]], writes=[P_b])
                    sc.op("act", lambda e: e.activation(out=P[0:NS, 16 * NS:17 * NS], in_=ps[3][0:NS, 16 * NS:17 * NS],
                                                        func=AF.Exp), reads=[pb[3]], writes=[P_b])
                    sc.op("dve", lambda e: e.tensor_tensor(out=P[:, 0:16 * NS], in0=P[:, 0:16 * NS],
                                                           in1=smask[:, 0:16 * NS], op=ALU.mult),
                          reads=[P_b, cb], writes=[P_b])
                    sc.op("dve", lambda e: e.tensor_tensor(out=P[0:NS, 16 * NS:17 * NS], in0=P[0:NS, 16 * NS:17 * NS],
                                                           in1=smask[0:NS, 16 * NS:17 * NS], op=ALU.mult),
                          reads=[P_b, cb], writes=[P_b])
                    for (bk, isz) in ((4, False), (5, True)):
                        for kb in range(16):
                            lhs = ones_b[:, :] if isz else Vc[:, kb * 512 + hh * 128: kb * 512 + (hh + 1) * 128]
                            sc.op("pe", lambda e, bk=bk, kb=kb, lhs=lhs: e.matmul(
                                ps[bk][:, 0:NS], lhs, P[:, kb * NS:(kb + 1) * NS], start=(kb == 0), stop=False),
                                reads=[P_b, c_b, cb], writes=[pb[bk]], signal=False)
                        lhs = ones_b[0:NS, :] if isz else Vs_sb[0:NS, hd * 128:(hd + 1) * 128]
                        sc.op("pe", lambda e, bk=bk, lhs=lhs: e.matmul(ps[bk][:, 0:NS], lhs, P[0:NS, 16 * NS:17 * NS],
                                                                      start=False, stop=True),
                              reads=[P_b, vs_b, cb], writes=[pb[bk]])
                    sc.op("dve", lambda e: e.reciprocal(out=rzs[:], in_=ps[5][:, 0:NS]), reads=[pb[5]], writes=[rz_b])
                    sc.op("dve", lambda e: e.tensor_tensor(out=H.brTs[1][:, hd * NS:(hd + 1) * NS], in0=ps[4][:, 0:NS],
                                                           in1=rzs[:], op=ALU.mult),
                          reads=[pb[4], rz_b], writes=[H.brTs_b[1]])
    sc.barrier()


def _mlstm_run(G, H, es, ntok, Lc, nch, igT, lfT, m0_src, st_src, st_dst, m_dst, loaders, co_store, hng, keep, sel):
    nc, sc, ps, pb = G.nc, G.sc, G.ps, G.pb
    sbt = H.sbt
    ident_f, ident_b, caus_f, cb = G.ident_f, G.ident_b, G.caus_f, G.cb
    gb = Buf()

    def g8(name, n=ntok):
        return sbt(es, name, [8, n], F32)
    m0 = g8("m0", 1)
    if m0_src is None:
        sc.op("dve", lambda e: e.memset(m0[:], 0.0), writes=[gb])
    else:
        sc.dma("sp", m0[:], m0_src, writes=[gb])
    mT, bT, egT, s1T, s2T, emT, uT = (g8("mT"), g8("bT"), g8("egT"), g8("s1T"), g8("s2T"), g8("emT"), g8("uT"))
    mprev, bend, mnew, scdc = g8("mprev", nch), g8("bend", nch), g8("mnew", nch), g8("scdc", 2 * nch)
    V = lambda e: e
    sc.op("dve", lambda e: e.tensor_tensor_scan(out=mT[:], data0=lfT[:, 0:ntok], data1=igT[:, 0:ntok], initial=m0[:, 0:1],
                                                op0=ALU.add, op1=ALU.max), reads=[H.gate_b, gb], writes=[gb])
    sc.op("dve", lambda e: e.tensor_tensor_scan(out=bT[:], data0=keep[:, 0:ntok], data1=lfT[:, 0:ntok], initial=0.0,
                                                op0=ALU.mult, op1=ALU.add), reads=[H.gate_b, gb], writes=[gb])
    sc.op("dve", lambda e: e.tensor_copy(out=mprev[:, 0:1], in_=m0[:, 0:1]), reads=[gb], writes=[gb])
    if nch > 1:
        sc.op("dve", lambda e: e.tensor_copy(
            out=mprev[:, 1:nch], in_=mT[:].rearrange("p (c t) -> p c t", t=Lc)[:, 0:nch - 1, Lc - 1]),
            reads=[gb], writes=[gb])
    sc.op("dve", lambda e: e.tensor_copy(out=bend[:], in_=bT[:].rearrange("p (c t) -> p c t", t=Lc)[:, :, Lc - 1]),
          reads=[gb], writes=[gb])
    sc.op("dve", lambda e: e.tensor_copy(out=mnew[:], in_=mT[:].rearrange("p (c t) -> p c t", t=Lc)[:, :, Lc - 1]),
          reads=[gb], writes=[gb])
    sc.dma("sp", m_dst, mT[:, ntok - 1:ntok], reads=[gb])
    sc.op("dve", lambda e: e.tensor_tensor(out=egT[:], in0=igT[:, 0:ntok], in1=bT[:], op=ALU.subtract),
          reads=[H.gate_b, gb], writes=[gb])
    sc.op("act", lambda e: e.activation(out=egT[:], in_=egT[:], func=AF.Exp), reads=[gb], writes=[gb])
    sc.op("dve", lambda e: e.tensor_tensor(out=uT[:], in0=bT[:], in1=mT[:], op=ALU.subtract), reads=[gb], writes=[gb])
    sc.op("act", lambda e: e.activation(out=s1T[:], in_=uT[:], func=AF.Exp), reads=[gb], writes=[gb])
    sc.op("dve", lambda e: e.tensor_tensor(
        out=s2T[:].rearrange("p (c t) -> p c t", t=Lc), in0=uT[:].rearrange("p (c t) -> p c t", t=Lc),
        in1=mprev[:, :].unsqueeze(2).broadcast_to([8, nch, Lc]), op=ALU.add), reads=[gb], writes=[gb])
    sc.op("act", lambda e: e.activation(out=s2T[:], in_=s2T[:], func=AF.Exp), reads=[gb], writes=[gb])
    sc.op("act", lambda e: e.activation(out=emT[:], in_=mT[:], func=AF.Exp, scale=-1.0), reads=[gb], writes=[gb])
    sc.op("dve", lambda e: e.tensor_tensor(out=scdc[:, 0:nch], in0=bend[:], in1=mnew[:], op=ALU.subtract),
          reads=[gb], writes=[gb])
    sc.op("dve", lambda e: e.tensor_tensor(out=scdc[:, nch:2 * nch], in0=scdc[:, 0:nch], in1=mprev[:], op=ALU.add),
          reads=[gb], writes=[gb])
    sc.op("act", lambda e: e.activation(out=scdc[:], in_=scdc[:], func=AF.Exp), reads=[gb], writes=[gb])
    cols = sbt(es, "mcols", [128, nch * 32], F32)
    bcs = sbt(es, "mbcs", [128, 8 * 2 * nch], F32)
    for c in range(nch):
        for qi, Q in enumerate((egT, s1T, s2T, emT)):
            sc.op("pe", lambda e, c=c, qi=qi, Q=Q: e.matmul(
                ps[5][0:Lc, (c * 4 + qi) * 8:(c * 4 + qi + 1) * 8], Q[:, c * Lc:(c + 1) * Lc], ident_f[0:8, 0:8],
                start=True, stop=True), reads=[gb, cb], writes=[pb[5]])
    sc.op("act", lambda e: e.activation(out=cols[0:Lc, :], in_=ps[5][0:Lc, 0:nch * 32], func=AF.Copy),
          reads=[pb[5]], writes=[gb])
    for hd in range(8):
        sc.op("pe", lambda e, hd=hd: e.matmul(ps[6][:, hd * 2 * nch:(hd + 1) * 2 * nch], sel[:, hd * 128:(hd + 1) * 128],
                                              scdc[:, :], start=True, stop=True), reads=[gb], writes=[pb[6]])
    sc.op("act", lambda e: e.activation(out=bcs[:], in_=ps[6][:, 0:16 * nch], func=AF.Copy), reads=[pb[6]], writes=[gb])

    Cst = sbt(es, "mCst", [128, 257], F32)
    Cbf = sbt(es, "mCbf", [128, 257], BF16)
    C_b = Buf()
    PT = sbt(es, "mPT", [128, 128], BF16)
    ktl = sbt(es, "mktl", [128, 128], BF16)
    hn1 = sbt(es, "mhn1", [128, 257], F32)
    hn2 = sbt(es, "mhn2", [128, 257], F32)
    dn = sbt(es, "mdn", [128, 4], F32)
    junk = sbt(es, "mjunk", [128, 256], F32)
    hnn = sbt(es, "mhnn", [128, 256], BF16)
    w_b = Buf()
    for hd in range(8):
        qT_t, kT_t, kc_t, vx_t, og_t, in_b = loaders(hd)
        if st_src is None:
            sc.op("dve", lambda e: e.memset(Cst[:], 0.0), writes=[C_b])
        else:
            sc.dma("sp", Cst[:], st_src[hd], writes=[C_b])
        sc.op("act", lambda e: e.activation(out=Cbf[:], in_=Cst[:], func=AF.Copy), reads=[C_b], writes=[C_b])
        for c in range(nch):
            tsl = slice(c * Lc, (c + 1) * Lc)
            col = lambda qi: cols[0:Lc, (c * 4 + qi) * 8 + hd:(c * 4 + qi) * 8 + hd + 1]
            sc.op("pe", lambda e: e.matmul(ps[0][0:Lc, 0:Lc], kT_t[:, tsl], qT_t[:, tsl], start=True, stop=True),
                  reads=[in_b], writes=[pb[0]])
            sc.op("dve", lambda e: e.scalar_tensor_tensor(out=PT[0:Lc, 0:Lc], in0=ps[0][0:Lc, 0:Lc], scalar=col(0),
                                                          in1=caus_f[0:Lc, 0:Lc], op0=ALU.mult, op1=ALU.mult),
                  reads=[pb[0], gb, cb], writes=[w_b])
            sc.op("act", lambda e: e.activation(out=ktl[0:Lc, :], in_=kc_t[0:Lc, c * 128:(c + 1) * 128], func=AF.Copy,
                                                scale=col(0)), reads=[in_b, gb], writes=[w_b])
            sc.op("pe", lambda e: e.matmul(ps[1][0:Lc, 0:257], PT[0:Lc, 0:Lc], vx_t[0:Lc, c * 257:(c + 1) * 257],
                                           start=True, stop=True), reads=[w_b, in_b], writes=[pb[1]])
            sc.op("pe", lambda e: e.matmul(ps[2][0:Lc, 0:257], qT_t[:, tsl], Cbf[:, :], start=True, stop=True),
                  reads=[in_b, C_b], writes=[pb[2]])
            sc.op("dve", lambda e: e.tensor_scalar(out=hn1[0:Lc, :], in0=ps[1][0:Lc, 0:257], scalar1=col(1), scalar2=None,
                                                   op0=ALU.mult), reads=[pb[1], gb], writes=[w_b])
            sc.op("dve", lambda e: e.scalar_tensor_tensor(out=hn2[0:Lc, :], in0=ps[2][0:Lc, 0:257], scalar=col(2),
                                                          in1=hn1[0:Lc, :], op0=ALU.mult, op1=ALU.add),
                  reads=[pb[2], gb, w_b], writes=[w_b])
            sc.op("act", lambda e: e.activation(out=dn[0:Lc, 0:1], in_=hn2[0:Lc, 256:257], func=AF.Abs),
                  reads=[w_b, gb], writes=[w_b])
            sc.op("dve", lambda e: e.tensor_tensor(out=dn[0:Lc, 0:1], in0=dn[0:Lc, 0:1], in1=col(3), op=ALU.max),
                  reads=[w_b, gb], writes=[w_b])
            sc.op("dve", lambda e: e.reciprocal(out=dn[0:Lc, 0:1], in_=dn[0:Lc, 0:1]), reads=[w_b], writes=[w_b])
            sc.op("act", lambda e: e.activation(out=junk[0:Lc, :], in_=hn2[0:Lc, 0:256], func=AF.Square,
                                                scale=dn[0:Lc, 0:1], accum_out=dn[0:Lc, 1:2]),
                  reads=[w_b], writes=[w_b])
            sc.op("act", lambda e: e.activation(out=dn[0:Lc, 1:2], in_=dn[0:Lc, 1:2], func=AF.Sqrt, bias=EPS,
                                                scale=1.0 / 256), reads=[w_b], writes=[w_b])
            sc.op("dve", lambda e: e.reciprocal(out=dn[0:Lc, 1:2], in_=dn[0:Lc, 1:2]), reads=[w_b], writes=[w_b])
            sc.op("dve", lambda e: e.tensor_tensor(out=dn[0:Lc, 2:3], in0=dn[0:Lc, 0:1], in1=dn[0:Lc, 1:2], op=ALU.mult),
                  reads=[w_b], writes=[w_b])
            sc.op("dve", lambda e: e.scalar_tensor_tensor(out=hnn[0:Lc, :], in0=hn2[0:Lc, 0:256], scalar=dn[0:Lc, 2:3],
                                                          in1=hng[0:Lc, hd * 256:(hd + 1) * 256], op0=ALU.mult,
                                                          op1=ALU.mult), reads=[w_b, gb], writes=[w_b])
            for a in range(2):
                sc.op("pe", lambda e, a=a: e.matmul(ps[3][:, a * Lc:(a + 1) * Lc], hnn[0:Lc, a * 128:(a + 1) * 128],
                                                    ident_b[0:Lc, 0:Lc], start=True, stop=True),
                      reads=[w_b, cb], writes=[pb[3]])
            co_store(hd, c, ps[3], pb[3], og_t, in_b)
            sc.op("pe", lambda e: e.matmul(ps[4][:, 0:257], ktl[0:Lc, :], vx_t[0:Lc, c * 257:(c + 1) * 257],
                                           start=True, stop=True), reads=[w_b, in_b], writes=[pb[4]])
            dcc = bcs[:, hd * 2 * nch + nch + c: hd * 2 * nch + nch + c + 1]
            scc = bcs[:, hd * 2 * nch + c: hd * 2 * nch + c + 1]
            sc.op("dve", lambda e: e.tensor_scalar(out=Cst[:], in0=Cst[:], scalar1=dcc, scalar2=None, op0=ALU.mult),
                  reads=[gb, C_b], writes=[C_b])
            sc.op("dve", lambda e: e.scalar_tensor_tensor(out=Cst[:], in0=ps[4][:, 0:257], scalar=scc, in1=Cst[:],
                                                          op0=ALU.mult, op1=ALU.add), reads=[pb[4], gb, C_b], writes=[C_b])
            sc.op("act", lambda e: e.activation(out=Cbf[:], in_=Cst[:], func=AF.Copy), reads=[C_b], writes=[C_b])
        sc.dma("sp", st_dst[hd], Cst[:], reads=[C_b])


def stage_mlstm(G, H):
    nc, sc, ws, ps, pb = G.nc, G.sc, G.ws, G.ps, G.pb
    l, h, tok0, do_s, sbt = G.l, H.h, H.tok0, H.do_s, H.sbt
    with contextlib.ExitStack() as es:
        hng = sbt(es, "hng", [128, 2048], F32)
        keep = sbt(es, "keep", [8, 1024], F32)
        sel = sbt(es, "sel", [8, 1024], F32)
        k_b = Buf()
        sc.dma("sp", hng[:], G.hng_in[l], writes=[H.gate_b])
        sc.dma("sp", keep[:], G.c_keep[:, :], writes=[H.gate_b])
        sc.dma("sp", sel[:], G.c_sel[:, :], writes=[H.gate_b])
        with contextlib.ExitStack() as e2:
            qT_t = [sbt(e2, "mq%d" % i, [128, TH], BF16) for i in range(2)]
            kT_t = [sbt(e2, "mk%d" % i, [128, TH], BF16) for i in range(2)]
            kc_t = [sbt(e2, "mkc%d" % i, [128, TH], BF16) for i in range(2)]
            vx_t = [sbt(e2, "mvx%d" % i, [128, NSUB * 257], BF16) for i in range(2)]
            og_t = [sbt(e2, "mog%d" % i, [128, 2 * TH], BF16) for i in range(2)]
            co = [sbt(e2, "mco%d" % i, [128, 2 * TH], BF16) for i in range(2)]
            in_b = bufs(2)
            co_b = bufs(2)
            for i in range(2):
                sc.op("dve", lambda e, i=i: e.memset(vx_t[i][:], 1.0), writes=[in_b[i]])

            def loaders(hd):
                i = hd % 2
                tk = slice(tok0, tok0 + TH)
                sc.dma("sp", qT_t[i][:], G.qcT_d[hd * 128:(hd + 1) * 128, tk], writes=[in_b[i]])
                sc.dma("sp", kT_t[i][:], G.kcT_d[hd * 128:(hd + 1) * 128, tk], writes=[in_b[i]])
                sc.dma("sp", kc_t[i][:].rearrange("p (c k) -> p c k", c=NSUB),
                       G.kc_d[tk, hd * 128:(hd + 1) * 128].rearrange("(c p) k -> p c k", p=128), writes=[in_b[i]])
                sc.dma("sp", vx_t[i][:].rearrange("p (c v) -> p c v", c=NSUB)[:, :, 0:256],
                       G.vc_d[tk, hd * 256:(hd + 1) * 256].rearrange("(c p) v -> p c v", p=128), writes=[in_b[i]])
                sc.dma("sp", og_t[i][:].rearrange("p (a t) -> p a t", a=2),
                       G.ogT_d[hd * 256:(hd + 1) * 256, tk].rearrange("(a p) t -> p a t", p=128), writes=[in_b[i]])
                return qT_t[i], kT_t[i], kc_t[i], vx_t[i], og_t[i], in_b[i]

            def co_store(hd, c, pst, pstb, og, ib):
                i = hd % 2
                sc.op("dve", lambda e: e.tensor_tensor(
                    out=co[i][:].rearrange("p (a t) -> p a t", a=2)[:, :, c * 128:(c + 1) * 128],
                    in0=pst[:, 0:256].rearrange("p (a t) -> p a t", a=2),
                    in1=og[:].rearrange("p (a t) -> p a t", a=2)[:, :, c * 128:(c + 1) * 128], op=ALU.mult),
                    reads=[pstb, ib], writes=[co_b[i]])
                if c == NSUB - 1:
                    sc.dma("sp", G.brT_d[2][hd * 256:(hd + 1) * 256, tok0:tok0 + TH].rearrange("(a p) t -> p a t", p=128),
                           co[i][:].rearrange("p (a t) -> p a t", a=2), reads=[co_b[i]])
            st_src = None if h == 0 else [G.st_out[l, hd] for hd in range(8)]
            m_src = None if h == 0 else G.m_out[l]
            _mlstm_run(G, H, e2, TH, 128, NSUB, H.igT, H.lfT, m_src, st_src,
                       [G.st_out[l, hd] for hd in range(8)], G.m_out[l], loaders, co_store, hng, keep, sel)
        sc.barrier()
        if do_s:
            with contextlib.ExitStack() as e3:
                kcs = sbt(e3, "skc", [NS, 1024], BF16)
                vxs = sbt(e3, "svx", [NS, 8 * 257], BF16)
                s_b = Buf()
                sc.op("dve", lambda e: e.memset(vxs[:], 1.0), writes=[s_b])
                sc.dma("sp", kcs[:], G.kcs_d[:, :], writes=[s_b])
                sc.dma("sp", vxs[:].rearrange("p (c v) -> p c v", c=8)[:, :, 0:256],
                       G.vcs_d[:, :].rearrange("p (c v) -> p c v", c=8), writes=[s_b])

                def loaders_s(hd):
                    return (H.qcTs[:, hd * NS:(hd + 1) * NS], H.kcTs[:, hd * NS:(hd + 1) * NS],
                            kcs[:, hd * 128:(hd + 1) * 128], vxs[:, hd * 257:(hd + 1) * 257],
                            H.ogTs[:, hd * 2 * NS:(hd + 1) * 2 * NS], s_b)

                def co_store_s(hd, c, pst, pstb, og, ib):
                    sc.op("dve", lambda e: e.tensor_tensor(out=H.brTs[2][:, hd * 2 * NS:(hd + 1) * 2 * NS],
                                                           in0=pst[:, 0:2 * NS], in1=og, op=ALU.mult),
                          reads=[pstb, H.smp_b], writes=[H.brTs_b[2]])
                _mlstm_run(G, H, e3, NS, NS, 1, H.igTs, H.lfTs, G.m_in[l], [G.st_in[l, hd] for hd in range(8)],
                           [G.sts_out[l, hd] for hd in range(8)], G.ms_out[l], loaders_s, co_store_s, hng, keep, sel)
    sc.barrier()


def _rmw_epi(G, H, es, xt, xt_b, cnt):
    sc, ps, pb = G.sc, G.ps, G.pb
    tok0, h, do_s = H.tok0, H.h, H.do_s

    def epi(sec, f, m, bA, bB, bS):
        fb = f // 128
        i = cnt["x"] % len(xt)
        cnt["x"] += 1
        yb = G.yT_b[fb][h]
        sc.dma("sp", xt[i][:], G.yT[f:f + 128, tok0:tok0 + TH], reads=[yb], writes=[xt_b[i]])
        for bk, blk in ((bA, 0), (bB, 1)):
            sl = slice(blk * 512, (blk + 1) * 512)
            sc.op("dve", lambda e, bk=bk, sl=sl: e.tensor_tensor(out=xt[i][:, sl], in0=ps[bk][:, :], in1=xt[i][:, sl],
                                                               op=ALU.add), reads=[pb[bk], xt_b[i]], writes=[xt_b[i]])
        sc.dma("sp", G.yT[f:f + 128, tok0:tok0 + TH], xt[i][:], reads=[xt_b[i]], writes=[yb])
        if do_s:
            sc.op("dve", lambda e: e.tensor_tensor(out=G.xsT[:, fb * NS:(fb + 1) * NS], in0=ps[bS][:, 0:NS],
                                                   in1=G.xsT[:, fb * NS:(fb + 1) * NS], op=ALU.add),
                  reads=[pb[bS], G.xsT_b], writes=[G.xsT_b])
    return epi


def stage_merge(G, H):
    nc, sc, ws, ps, pb = G.nc, G.sc, G.ws, G.ps, G.pb
    l, h, tok0, do_s, sbt = G.l, H.h, H.tok0, H.do_s, H.sbt
    with contextlib.ExitStack() as es:
        mT = sbt(es, "mT", [128, 32 * 1024], BF16)
        mT_b = bufs(2)
        mTs = sbt(es, "mTs", [128, 32 * NS], BF16)
        mTs_b = Buf()
        brT = sbt(es, "brT", [128, 16 * 1024], BF16)
        brT_b = bufs(2)
        gst = [sbt(es, "gst%d" % i, [128, 1024], BF16) for i in range(2)]
        gst_b = bufs(2)
        tmpm = sbt(es, "tmpm", [128, 512], BF16)
        tmpm_b = Buf()
        tmps = sbt(es, "tmpms", [128, NS], BF16)
        cnt = dict(g=0, x=0)
        for br in range(3):
            sc.dma("sp", H.kview(brT[:], 16),
                   G.brT_d[br][:, tok0:tok0 + TH].rearrange("(k p) t -> p k t", p=128),
                   writes=[brT_b[0], brT_b[1]])

            def epi(sec, f, m, bA, bB, bS, br=br):
                fb = f // 128
                i = cnt["g"] % 2
                cnt["g"] += 1
                sc.dma("sp", gst[i][:], G.gT_d[br * D + f: br * D + f + 128, tok0:tok0 + TH], writes=[gst_b[i]])
                for bk, blk in ((bA, 0), (bB, 1)):
                    sl = slice(blk * 512, (blk + 1) * 512)
                    dst = mT[:, fb * 1024 + blk * 512: fb * 1024 + (blk + 1) * 512]
                    if br == 0:
                        sc.op("dve", lambda e, bk=bk, sl=sl, dst=dst: e.tensor_tensor(
                            out=dst, in0=ps[bk][:, :], in1=gst[i][:, sl], op=ALU.mult),
                            reads=[pb[bk], gst_b[i]], writes=[mT_b[blk]])
                    else:
                        sc.op("dve", lambda e, bk=bk, sl=sl: e.tensor_tensor(
                            out=tmpm[:], in0=ps[bk][:, :], in1=gst[i][:, sl], op=ALU.mult),
                            reads=[pb[bk], gst_b[i]], writes=[tmpm_b])
                        sc.op("dve", lambda e, dst=dst: e.tensor_tensor(out=dst, in0=dst, in1=tmpm[:], op=ALU.add),
                              reads=[tmpm_b, mT_b[blk]], writes=[mT_b[blk]])
                if do_s:
                    gsl = H.gTs[:, (br * 32 + fb) * NS:(br * 32 + fb + 1) * NS]
                    dsts = mTs[:, fb * NS:(fb + 1) * NS]
                    if br == 0:
                        sc.op("dve", lambda e: e.tensor_tensor(out=dsts, in0=ps[bS][:, 0:NS], in1=gsl, op=ALU.mult),
                              reads=[pb[bS], H.smp_b], writes=[mTs_b])
                    else:
                        sc.op("dve", lambda e: e.tensor_tensor(out=tmps[:], in0=ps[bS][:, 0:NS], in1=gsl, op=ALU.mult),
                              reads=[pb[bS], H.smp_b], writes=[tmpm_b])
                        sc.op("dve", lambda e: e.tensor_tensor(out=dsts, in0=dsts, in1=tmps[:], op=ALU.add),
                              reads=[tmpm_b, mTs_b], writes=[mTs_b])

            wl = G.w_br[br][l].rearrange("(k p) n -> p k n", p=128)
            tiles = [("br", c, 512, "fm", c) for c in range(0, D, 512)]
            H.gemm(es, tiles, lambda c0, n, wl=wl: wl[:, :, c0:c0 + n], 16, brT, brT_b, H.brTs[br], H.brTs_b[br],
                   epi, None)
        xt = [sbt(es, "xt%d" % i, [128, 1024], F32) for i in range(2)]
        xt_b = bufs(2)
        wl = G.w_out[l].rearrange("(k p) n -> p k n", p=128)
        tiles = [("wo", c, 256, "fm", c) for c in range(0, D, 256)]
        H.gemm(es, tiles, lambda c0, n: wl[:, :, c0:c0 + n], 32, mT, mT_b, mTs, mTs_b,
               _rmw_epi(G, H, es, xt, xt_b, cnt), None)
    sc.barrier()


def stage_ffn(G, H):
    nc, sc, ws, ps, pb = G.nc, G.sc, G.ws, G.ps, G.pb
    l, h, tok0, do_s, sbt = G.l, H.h, H.tok0, H.do_s, H.sbt
    with contextlib.ExitStack() as es:
        xnT = sbt(es, "xn2T", [128, 32 * 1024], BF16)
        xnT_b = bufs(2)
        H.rmsnorm(es, H.g2, xnT, xnT_b, G.xnTs)
        hT = sbt(es, "hT", [128, 16 * 1024], BF16)
        hT_b = bufs(2)
        hTs = sbt(es, "hTs", [128, 16 * NS], BF16)
        hTs_b = Buf()
        rl = [sbt(es, "rl%d" % i, [128, 512], F32) for i in range(2)]
        rl_b = bufs(2)
        rls = sbt(es, "rls", [128, NS], F32)
        rls_b = Buf()
        xt = [sbt(es, "xt%d" % i, [128, 1024], F32) for i in range(2)]
        xt_b = bufs(2)
        cnt = dict(r=0, x=0)
        w1 = G.w_ff1[l].rearrange("(k p) n -> p k n", p=128)
        for kg in range(8):
            def epi1(sec, f, m, bA, bB, bS):
                fb = f // 128
                for bk, blk in ((bA, 0), (bB, 1)):
                    i = cnt["r"] % 2
                    cnt["r"] += 1
                    dst = hT[:, fb * 1024 + blk * 512: fb * 1024 + (blk + 1) * 512]
                    sc.op("act", lambda e, bk=bk, i=i: e.activation(out=rl[i][:], in_=ps[bk][:, :], func=AF.Relu),
                          reads=[pb[bk]], writes=[rl_b[i]])
                    sc.op("dve", lambda e, i=i, dst=dst: e.tensor_tensor(out=dst, in0=rl[i][:], in1=rl[i][:], op=ALU.mult),
                          reads=[rl_b[i]], writes=[hT_b[blk]])
                if do_s:
                    sc.op("act", lambda e: e.activation(out=rls[:], in_=ps[bS][:, 0:NS], func=AF.Relu),
                          reads=[pb[bS]], writes=[rls_b])
                    sc.op("dve", lambda e: e.tensor_tensor(out=hTs[:, fb * NS:(fb + 1) * NS], in0=rls[:], in1=rls[:],
                                                           op=ALU.mult), reads=[rls_b], writes=[hTs_b])
            tiles = [("f1", kg * 2048 + c, 256, "fm", c) for c in range(0, 2048, 256)]
            H.gemm(es, tiles, lambda c0, n: w1[:, :, c0:c0 + n], 32, xnT, xnT_b, G.xnTs, G.xnTs_b, epi1, None)
            w2 = G.w_ff2[l][kg * 2048:(kg + 1) * 2048, :].rearrange("(k p) n -> p k n", p=128)
            tiles = [("f2", c, 512, "fm", c) for c in range(0, D, 512)]
            H.gemm(es, tiles, lambda c0, n, w2=w2: w2[:, :, c0:c0 + n], 16, hT, hT_b, hTs, hTs_b,
                   _rmw_epi(G, H, es, xt, xt_b, cnt), None)
    sc.barrier()


_NC = None
_DBG = None


def kernel(x_prompt, x_sample, cache_swa_k, cache_swa_v, state_mlstm_C, state_mlstm_n, state_mlstm_m,
           norm1_g, w_in, ws_a, bs_a, norm_va_g, qn_g, kn_g, i_b, f_b, hn_c_g,
           w_branch_a, w_branch_b, w_branch_c, w_out, norm2_g, w_ff1, w_ff2):
    global _NC
    f32 = np.float32
    A = lambda a: np.ascontiguousarray(np.asarray(a, dtype=f32))
    if _NC is None:
        _NC = build()
    nc = _NC
    L = DEPTH
    consts = host_consts()
    shared = dict(
        w_in=A(w_in), w_b0=A(w_branch_a), w_b1=A(w_branch_b), w_b2=A(w_branch_c), w_out=A(w_out),
        w_ff1=A(w_ff1), w_ff2=A(w_ff2),
        g1=A(np.asarray(norm1_g).reshape(L, 32, 128).transpose(0, 2, 1)),
        g2=A(np.asarray(norm2_g).reshape(L, 32, 128).transpose(0, 2, 1)),
        gva=A(np.broadcast_to(np.asarray(norm_va_g)[:, None, :], (L, 128, 2048))),
        hng=A(np.broadcast_to(np.asarray(hn_c_g)[:, None, :], (L, 128, 2048))),
        gq=A(np.asarray(qn_g).reshape(L, 128, 1)), gk=A(np.asarray(kn_g).reshape(L, 128, 1)),
        ib=A(np.asarray(i_b).reshape(L, 8, 1)), fb=A(np.asarray(f_b).reshape(L, 8, 1)),
        wsT=A(np.asarray(ws_a).transpose(0, 3, 1, 2).reshape(L, 128, 1024)),
        bT=A(np.broadcast_to(np.asarray(bs_a)[:, None, :, :], (L, 128, 8, 128)).reshape(L, 128, 1024)),
    )
    shared.update(consts)
    xp = np.asarray(x_prompt)
    xs = np.asarray(x_sample)
    ck = np.asarray(cache_swa_k)
    cv = np.asarray(cache_swa_v)
    sC = np.asarray(state_mlstm_C)
    sn = np.asarray(state_mlstm_n)
    sm = np.asarray(state_mlstm_m)
    if KSTOP <= 10:
        shared["w_in"] = A(np.asarray(w_in)[0:1])
        for k_, a_ in (("w_b0", w_branch_a), ("w_b1", w_branch_b), ("w_b2", w_branch_c), ("w_out", w_out)):
            shared[k_] = A(np.asarray(a_)[0:1]) if KSTOP >= 6 else np.zeros((1, 1, 1), f32)
        for k_, a_ in (("w_ff1", w_ff1), ("w_ff2", w_ff2)):
            shared[k_] = A(np.asarray(a_)[0:1]) if KSTOP >= 7 else np.zeros((1, 1, 1), f32)
    in_maps = []
    for c in range(8):
        b = c % 4
        d = dict(shared)
        d["xT"] = A(xp[b].T)
        d["xsT"] = A(xs[c].T)
        d["ck"] = A(ck[:, c].reshape(L, 2048, 2048))
        d["cv"] = A(cv[:, c].reshape(L, 2048, 2048))
        d["st_in"] = A(np.concatenate([sC[:, c].transpose(0, 1, 3, 2), sn[:, c][..., None]], axis=-1))
        d["m_in"] = A(sm[:, c].reshape(L, 8, 1))
        in_maps.append(d)
    res = run_bass_kernel_spmd(nc, in_maps, core_ids=list(range(8)))
    R = res.results
    global _DBG
    _DBG = [r.get("dbg") for r in R]
    y_p = np.stack([R[b]["yT"].T for b in range(4)]).astype(f32)
    y_s = np.stack([R[c]["ysT"].T for c in range(8)]).astype(f32)
    k_p = np.stack([np.stack([R[b]["kT_out"][l].T.reshape(S, 16, 128) for b in range(4)]) for l in range(L)])
    v_p = np.stack([np.stack([R[b]["v_out"][l].reshape(S, 16, 128) for b in range(4)]) for l in range(L)])
    C_p = np.stack([np.stack([R[b]["st_out"][l][:, :, :256].transpose(0, 2, 1) for b in range(4)]) for l in range(L)])
    n_p = np.stack([np.stack([R[b]["st_out"][l][:, :, 256] for b in range(4)]) for l in range(L)])
    m_p = np.stack([np.stack([R[b]["m_out"][l][:, 0] for b in range(4)]) for l in range(L)])
    k_s = np.stack([np.stack([R[c]["ks_out"][l].T.reshape(NS, 16, 128) for c in range(8)]) for l in range(L)])
    v_s = np.stack([np.stack([R[c]["vs_out"][l].reshape(NS, 16, 128) for c in range(8)]) for l in range(L)])
    C_s = np.stack([np.stack([R[c]["sts_out"][l][:, :, :256].transpose(0, 2, 1) for c in range(8)]) for l in range(L)])
    n_s = np.stack([np.stack([R[c]["sts_out"][l][:, :, 256] for c in range(8)]) for l in range(L)])
    m_s = np.stack([np.stack([R[c]["ms_out"][l][:, 0] for c in range(8)]) for l in range(L)])
    va_s = np.stack([np.stack([R[c]["va_out"][l] for c in range(8)]) for l in range(L)])
    outs = (y_p, y_s, k_p, v_p, C_p, n_p, m_p, k_s, v_s, C_s, n_s, m_s, va_s)
    return tuple(np.ascontiguousarray(o, dtype=f32) for o in outs)
```
